# Optimizing a Trainium2 kernel written in Bass

```python
import math
import jax, jax.numpy as jnp
from jax import lax
import numpy as np

D_MODEL = 1024
BATCH = 16
SEQ = 2048
DEPTH = 1
DEC_BATCH = 8
DEC_SEQ = 32
PAST_LEN = 1024

CHUNK = 64
N_HEADS = 8
N_KV = 2
HEAD_DIM = 128
ATT_W = N_HEADS * HEAD_DIM
KV_W = N_KV * HEAD_DIM
ROT_FRAC = 4
ROPE_THETA = 500000.0
IDX_HEADS = 8
IDX_DIM = 64
TOPK_MAX = 256
CONV_W = 3
CONV_DIM = D_MODEL
Q_BLOCK = 128
EPS = 1e-6
SPLIT_SIZES = (ATT_W, KV_W, KV_W, ATT_W, IDX_HEADS * IDX_DIM, IDX_DIM, IDX_HEADS,
               CONV_DIM, CONV_DIM, CONV_DIM, CONV_DIM, D_MODEL, D_MODEL)
PROJ_W = 4 * ATT_W // 2 + 2 * KV_W + IDX_HEADS * IDX_DIM + IDX_DIM + IDX_HEADS + 4 * CONV_DIM + 2 * D_MODEL

kernel_name = "hybrid_streaming_dsa_shortconv_step"


def rms_norm(x, g):
    xf = x.astype(jnp.float32)
    y = xf * lax.rsqrt(jnp.mean(xf * xf, axis=-1, keepdims=True) + EPS)
    return (y * g.astype(jnp.float32)).astype(x.dtype)


def partial_rope(x, pos):
    dh = x.shape[-1]
    rot = dh // ROT_FRAC
    half = rot // 2
    inv = ROPE_THETA ** (-2.0 * jnp.arange(half, dtype=jnp.float32) / rot)
    ang = pos.astype(jnp.float32)[:, None] * inv[None, :]
    cos = jnp.cos(ang)[:, None, :]
    sin = jnp.sin(ang)[:, None, :]
    xr = x[..., :rot].astype(jnp.float32)
    x1, x2 = xr[..., :half], xr[..., half:]
    out = jnp.concatenate([x1 * cos - x2 * sin, x2 * cos + x1 * sin], axis=-1)
    return jnp.concatenate([out.astype(x.dtype), x[..., rot:]], axis=-1)


def sparse_attention(q, qi, wi, q_pos, k, v, ki, k_pos, topk):
    B, Q = q.shape[0], q.shape[1]
    logits = jnp.einsum('bqhd,bsd->bqhs', qi, ki).astype(jnp.float32) * (IDX_DIM ** -0.5)
    score = jnp.einsum('bqh,bqhs->bqs', wi.astype(jnp.float32) * (IDX_HEADS ** -0.5),
                       jax.nn.relu(logits))
    admissible = (k_pos[None, :] // CHUNK) <= (q_pos[:, None] // CHUNK)
    score = jnp.where(admissible[None], score, -jnp.inf)
    _, idx = lax.top_k(score, topk)
    valid = admissible[jnp.arange(Q)[None, :, None], idx]
    bidx = jnp.arange(B)[:, None, None]
    kg = k[bidx, idx]
    vg = v[bidx, idx]
    qg = q.reshape(B, Q, N_KV, N_HEADS // N_KV, HEAD_DIM)
    s = jnp.einsum('bqhgd,bqkhd->bqhgk', qg, kg).astype(jnp.float32) * (HEAD_DIM ** -0.5)
    s = jnp.where(valid[:, :, None, None, :], s, -jnp.inf)
    p = jax.nn.softmax(s, axis=-1).astype(v.dtype)
    o = jnp.einsum('bqhgk,bqkhd->bqhgd', p, vg)
    return o.reshape(B, Q, ATT_W)


def encoder_layer(x, c, pos, past_k, past_v, past_ki, conv_state,
                  w_ada, b_ada, norm_g, w_in, q_norm_g, k_norm_g, conv_w, w_pa, w_pb, w_out):
    B, T, _ = x.shape
    P = past_k.shape[1]
    mod = jax.nn.silu(c) @ w_ada + b_ada
    shift, scale, gate = jnp.split(mod, 3, axis=-1)
    h = rms_norm(x, norm_g) * (1.0 + scale[:, None, :]) + shift[:, None, :]
    proj = h @ w_in
    (q, k, v, za, qi, ki, wi, u, bg, cg, zb, ga, gb) = jnp.split(
        proj, list(np.cumsum(SPLIT_SIZES)[:-1]), axis=-1)
    q = partial_rope(rms_norm(q.reshape(B, T, N_HEADS, HEAD_DIM), q_norm_g), pos)
    k = partial_rope(rms_norm(k.reshape(B, T, N_KV, HEAD_DIM), k_norm_g), pos)
    v = v.reshape(B, T, N_KV, HEAD_DIM)
    qi = partial_rope(qi.reshape(B, T, IDX_HEADS, IDX_DIM), pos)
    ki = partial_rope(ki[:, :, None, :], pos)[:, :, 0, :]

    k_all = jnp.concatenate([past_k, k], axis=1)
    v_all = jnp.concatenate([past_v, v], axis=1)
    ki_all = jnp.concatenate([past_ki, ki], axis=1)
    L = P + T
    k_pos = jnp.arange(L, dtype=jnp.int32)
    topk = min(TOPK_MAX, L // 4)
    qb = min(Q_BLOCK, T)
    nb = T // qb

    def to_blocks(a):
        return jnp.moveaxis(a.reshape((B, nb, qb) + a.shape[2:]), 1, 0)

    o = lax.map(lambda args: sparse_attention(args[0], args[1], args[2], args[3],
                                              k_all, v_all, ki_all, k_pos, topk),
                (to_blocks(q), to_blocks(qi), to_blocks(wi), pos.reshape(nb, qb)))
    o = jnp.moveaxis(o, 0, 1).reshape(B, T, ATT_W)
    a_out = (o * jax.nn.silu(za)) @ w_pa

    cv = cg * u
    cp = jnp.concatenate([conv_state, cv], axis=1)
    y_conv = conv_w[0] * cp[:, 0:T]
    for j in range(1, CONV_W):
        y_conv = y_conv + conv_w[j] * cp[:, j:j + T]
    b_out = (bg * y_conv * jax.nn.silu(zb)) @ w_pb
    new_conv = cp[:, -(CONV_W - 1):]

    merged = jax.nn.sigmoid(ga) * a_out + jax.nn.sigmoid(gb) * b_out
    y = x + gate[:, None, :] * (merged @ w_out)
    return y, k, v, ki, new_conv


def setup_inputs(seed: int = 0) -> dict:
    key = jax.random.key(seed)
    ks = jax.random.split(key, 20)
    f = jnp.float32
    nrm = lambda k, shape, s: jax.random.normal(k, shape, f) * s
    return {
        "x_prompt": nrm(ks[0], (BATCH, SEQ, D_MODEL), 1.0),
        "x_sample": nrm(ks[1], (DEC_BATCH, DEC_SEQ, D_MODEL), 1.0),
        "cache_k": nrm(ks[2], (DEC_BATCH, PAST_LEN, N_KV, HEAD_DIM), 1.0),
        "cache_v": nrm(ks[3], (DEC_BATCH, PAST_LEN, N_KV, HEAD_DIM), 1.0),
        "cache_idx_k": nrm(ks[4], (DEC_BATCH, PAST_LEN, IDX_DIM), 1.0),
        "state_conv": nrm(ks[5], (DEC_BATCH, CONV_W - 1, CONV_DIM), 1.0),
        "c_prompt": nrm(ks[6], (BATCH, D_MODEL), 1.0),
        "c_sample": nrm(ks[7], (DEC_BATCH, D_MODEL), 1.0),
        "w_ada": nrm(ks[8], (D_MODEL, 3 * D_MODEL), 0.5 * D_MODEL ** -0.5),
        "b_ada": nrm(ks[9], (3 * D_MODEL,), 0.01),
        "norm_g": 1.0 + nrm(ks[10], (D_MODEL,), 0.02),
        "w_in": nrm(ks[11], (D_MODEL, PROJ_W), D_MODEL ** -0.5),
        "q_norm_g": 1.0 + nrm(ks[12], (HEAD_DIM,), 0.02),
        "k_norm_g": 1.0 + nrm(ks[13], (HEAD_DIM,), 0.02),
        "conv_w": nrm(ks[14], (CONV_W, CONV_DIM), CONV_W ** -0.5),
        "w_pa": nrm(ks[15], (ATT_W, D_MODEL), ATT_W ** -0.5),
        "w_pb": nrm(ks[16], (CONV_DIM, D_MODEL), CONV_DIM ** -0.5),
        "w_out": nrm(ks[17], (D_MODEL, D_MODEL), D_MODEL ** -0.5),
    }


def reference(x_prompt, x_sample, cache_k, cache_v, cache_idx_k, state_conv, c_prompt, c_sample,
              w_ada, b_ada, norm_g, w_in, q_norm_g, k_norm_g, conv_w, w_pa, w_pb, w_out):
    dt = x_prompt.dtype
    yp = x_prompt
    for _ in range(DEPTH):
        yp, k_p, v_p, ki_p, conv_p = encoder_layer(
            yp, c_prompt, jnp.arange(SEQ, dtype=jnp.int32),
            jnp.zeros((BATCH, 0, N_KV, HEAD_DIM), dt), jnp.zeros((BATCH, 0, N_KV, HEAD_DIM), dt),
            jnp.zeros((BATCH, 0, IDX_DIM), dt), jnp.zeros((BATCH, CONV_W - 1, CONV_DIM), dt),
            w_ada, b_ada, norm_g, w_in, q_norm_g, k_norm_g, conv_w, w_pa, w_pb, w_out)
    ys = x_sample
    for _ in range(DEPTH):
        ys, k_s, v_s, ki_s, conv_s = encoder_layer(
            ys, c_sample, PAST_LEN + jnp.arange(DEC_SEQ, dtype=jnp.int32),
            cache_k, cache_v, cache_idx_k, state_conv,
            w_ada, b_ada, norm_g, w_in, q_norm_g, k_norm_g, conv_w, w_pa, w_pb, w_out)
    return (yp, ys, k_p, v_p, ki_p, conv_p, k_s, v_s, ki_s, conv_s)
```

```python
import numpy as np
import ml_dtypes
import concourse.bass as bass
import concourse.mybir as mybir
from concourse.bass_utils import run_bass_kernel_spmd

F32 = mybir.dt.float32
BF16 = mybir.dt.bfloat16
ALU = mybir.AluOpType
AF = mybir.ActivationFunctionType
AX = mybir.AxisListType

D = 1024
SEQ = 2048
DEC_SEQ = 32
PAST = 1024
PROJ_W = 9288
NIT = 20
TOPK = 256.0
NEG = -30000.0
NEGBIG = -1.0e30
SCALE = 128.0 ** -0.5
WSC = (64.0 ** -0.5) * (8.0 ** -0.5)
EPS = 1e-6
SEQUENTIAL_RR = False
STRICT_SYNC = False
TRUNC = ''


class StopBuild(Exception):
    pass

NPT = 17

O_COS16 = 0
O_SIN16 = O_COS16 + NPT * 16
O_COS8 = O_SIN16 + NPT * 16
O_SIN8 = O_COS8 + NPT * 8
O_POW2 = O_SIN8 + NPT * 8
O_GQ = O_POW2 + (NIT + 2)
O_GK = O_GQ + 128
O_BADA = O_GK + 128
O_NG = O_BADA + 24
O_CW = O_NG + 8
O_NH = O_CW + 24
NPF = O_NH + 16


class Buf:
    __slots__ = ("name", "lw", "rd", "dsem", "dval", "dkey", "psum")

    def __init__(self, name):
        self.name = name
        self.psum = False
        self.lw = None
        self.rd = {}
        self.dsem = None
        self.dval = 0
        self.dkey = None


class Eng:
    def __init__(self, k, name, obj):
        self.name = name
        self.obj = obj
        self.sem = k.nc.alloc_semaphore("e_" + name)
        self.cnt = 0
        self.pending = False
        self.seen = {}


class K:
    def __init__(self, nc):
        self.nc = nc
        self.E = {}
        for n, o in (("pe", nc.tensor), ("act", nc.scalar), ("dve", nc.vector),
                     ("pool", nc.gpsimd), ("sp", nc.sync)):
            self.E[n] = Eng(self, n, o)
        self.dbufs = []

    def buf(self, name):
        return Buf(name)

    def bufs(self, name, n):
        return [Buf("%s%d" % (name, i)) for i in range(n)]

    def _wait(self, e, h):
        if h is None:
            return
        if h[0] == "e":
            key = h[1]
            sem = self.E[h[1]].sem
        else:
            key = h[1]
            sem = h[3]
        val = h[2]
        if e.seen.get(key, 0) >= val:
            return
        e.obj.wait_ge(sem, val)
        e.seen[key] = val

    def _deps(self, en, R, W, dkey=None):
        e = self.E[en]
        for b in R:
            if b.psum:
                for key_, h in b.rd.items():
                    if key_ != en:
                        self._wait(e, h)
            h = b.lw
            if h is None:
                continue
            if h[0] == "e" and h[1] == en and en == "pe":
                continue
            self._wait(e, h)
        for b in W:
            for j, h in enumerate([b.lw] + list(b.rd.values())):
                if h is None:
                    continue
                if h[0] == "e" and h[1] == en and (en in ("pe", "sp") or (not STRICT_SYNC and en != "pool")):
                    continue
                if j == 0 and h[0] == "d" and dkey is not None and h[1] == dkey:
                    continue
                self._wait(e, h)

    def I(self, en, fn, R=(), W=(), sig=True):
        e = self.E[en]
        self._deps(en, R, W)
        inst = fn(e.obj)
        if sig:
            e.cnt += 1
            inst.then_inc(e.sem, 1)
            e.pending = False
            h = ("e", en, e.cnt)
        else:
            e.pending = True
            h = ("e", en, e.cnt + 1)
        for b in W:
            b.lw = h
            b.rd = {}
        for b in R:
            b.rd[en] = h
        return inst

    def dma(self, qn, out_ap, in_ap, R=(), W=(), sb=None, **kw):
        e = self.E[qn]
        if sb.dsem is None:
            sb.dkey = "d%d" % len(self.dbufs)
            sb.dsem = self.nc.alloc_semaphore(sb.dkey)
            self.dbufs.append(sb)
        self._deps(qn, R, W, dkey=sb.dkey)
        inst = e.obj.dma_start(out=out_ap, in_=in_ap, **kw)
        sb.dval += 16
        inst.then_inc(sb.dsem, 16)
        h = ("d", sb.dkey, sb.dval, sb.dsem)
        for b in W:
            b.lw = h
            b.rd = {}
        for b in R:
            b.rd[sb.dkey] = h
        return inst

    def finish(self, en="sp"):
        e = self.E[en]
        assert not self.E["pe"].pending
        for sb in self.dbufs:
            self._wait(e, ("d", sb.dkey, sb.dval, sb.dsem))


def build_program():
    nc = bass.Bass("TRN2", target_bir_lowering=False)
    k = K(nc)

    def din(name, shape, dt=F32):
        return nc.dram_tensor(name, list(shape), dt, kind="ExternalInput").ap()

    def dout(name, shape, dt=F32):
        return nc.dram_tensor(name, list(shape), dt, kind="ExternalOutput").ap()

    def sb(name, shape, dt=F32):
        return nc.alloc_sbuf_tensor("s_" + name, list(shape), dt)

    xp = din("xp", [2, SEQ, D])
    xs = din("xs", [DEC_SEQ, D])
    ck = din("ck", [PAST, 256])
    cvv = din("cvv", [PAST, 256])
    cik = din("cik", [PAST, 64])
    sconv = din("sconv", [128, 8, 2])
    cT_d = din("cT", [128, 8, 3])
    w_ada = din("w_ada", [D, 3 * D])
    w_in = din("w_in", [D, PROJ_W])
    w_pa = din("w_pa", [D, D])
    w_pb = din("w_pb", [D, D])
    w_out = din("w_out", [D, D])
    pf_d = din("pf32", [128, NPF])
    pb_d = din("pbf", [128, 768], BF16)
    sel_d = din("sel", [3, 3 * 128])
    bgate_d = din("bgate", [3, D])

    y_p = dout("y_p", [2, SEQ, D])
    y_s = dout("y_s", [DEC_SEQ, D])
    k_p = dout("k_p", [2, SEQ, 256])
    v_p = dout("v_p", [2, SEQ, 256])
    ki_p = dout("ki_p", [2, SEQ, 64])
    conv_p = dout("conv_p", [2, 128, 8, 2])
    k_s = dout("k_s", [DEC_SEQ, 256])
    v_s = dout("v_s", [DEC_SEQ, 256])
    ki_s = dout("ki_s", [DEC_SEQ, 64])
    conv_s = dout("conv_s", [128, 8, 2])

    OFF = {"q": 0, "k": 1024, "v": 1280, "za": 1536, "qi": 2560, "ki": 3072, "wi": 3136,
           "u": 3144, "bg": 4168, "cg": 5192, "zb": 6216, "ga": 7240, "gb": 8264}
    blocks = {}
    Bw_src = k.buf("wsrc")
    grp = {g: k.buf("wg_" + g) for g in ("ada", "tm", "fm", "tail")}

    def add_block(name, src, g):
        ncols = src.shape[1]
        scr = nc.dram_tensor("scr_" + name, [D, ncols], BF16, kind="Internal").ap()
        k.dma("pool", scr, src, R=[Bw_src], W=[grp[g]], sb=grp[g])
        blocks[name] = (scr, grp[g], ncols)

    for j in range(6):
        add_block("ada%d" % j, w_ada[:, j * 512:(j + 1) * 512], "ada")
    add_block("q0", w_in[:, 0:512], "tm")
    add_block("q1", w_in[:, 512:1024], "tm")
    add_block("kv", w_in[:, 1024:1536], "tm")
    add_block("qi", w_in[:, 2560:3072], "tm")
    add_block("kiwi", w_in[:, 3072:3144], "tm")
    for nm in ("za",):
        for hf in range(2):
            add_block("%s%d" % (nm, hf), w_in[:, OFF[nm] + hf * 512: OFF[nm] + (hf + 1) * 512], "fm")
    for hf in range(2):
        for nm in ("u", "cg", "bg", "zb"):
            add_block("%s%d" % (nm, hf), w_in[:, OFF[nm] + hf * 512: OFF[nm] + (hf + 1) * 512], "fm")
    for hf in range(2):
        add_block("pa%d" % hf, w_pa[:, hf * 512:(hf + 1) * 512], "tail")
    for nm in ("ga", "gb"):
        for hf in range(2):
            add_block("%s%d" % (nm, hf), w_in[:, OFF[nm] + hf * 512: OFF[nm] + (hf + 1) * 512], "tail")
    for hf in range(2):
        add_block("pb%d" % hf, w_pb[:, hf * 512:(hf + 1) * 512], "tail")
    for hf in range(2):
        add_block("out%d" % hf, w_out[:, hf * 512:(hf + 1) * 512], "tail")

    PASS_BLOCKS = (["q0", "q1", "kv", "qi", "kiwi", "za0", "za1"]
                   + ["%s%d" % (nm, hf) for hf in range(2) for nm in ("u", "cg", "bg", "zb")]
                   + ["gb0", "pb0", "gb1", "pb1", "ga0", "pa0", "ga1", "pa1", "out0", "out1"])
    NPASS = 9
    stream = ["ada%d" % j for j in range(6)] + PASS_BLOCKS * NPASS

    NRING = 3
    ring = [sb("ring%d" % i, [128, 8, 512], BF16) for i in range(NRING)]
    Bring = k.bufs("ring", NRING)
    st = {"issued": 0, "pos": 0}

    def issue_to(n):
        while st["issued"] < min(n, len(stream)):
            i = st["issued"]
            scr, g, ncols = blocks[stream[i]]
            s = i % NRING
            k.dma("sp", ring[s][:, :, 0:ncols], scr.rearrange("(kc p) c -> p kc c", p=128),
                  R=[g], W=[Bring[s]], sb=Bring[s])
            st["issued"] += 1

    def next_block(expect):
        i = st["pos"]
        assert stream[i] == expect, (stream[i], expect)
        issue_to(i + NRING)
        st["pos"] += 1
        return ring[i % NRING], Bring[i % NRING]

    pf = sb("pf", [128, NPF])
    pbc = sb("pbc", [128, 768], BF16)
    sel = sb("sel", [3, 384])

    cT = sb("cT", [128, 8, 3])
    scT = sb("scT", [128, 8, 3], BF16)
    cth = sb("cth", [128, 8, 3])
    modfm = sb("modfm", [128, 16, 3])
    Amod = sb("Amod", [128, 8, 3])
    gate_tm = sb("gate_tm", [3, D])
    gate_bc = sb("gate_bc", [128, D])
    xt = [sb("xt%d" % i, [128, D]) for i in range(2)]

    hT = sb("hT", [128, 8, 512], BF16)
    QT = sb("QT", [128, 8, 512], BF16)
    KT = sb("KT", [128, 2, SEQ], BF16)
    Vb = sb("Vb", [128, 16, 256], BF16)
    kiT = sb("kiT", [128, SEQ], BF16)
    qiT = sb("qiT", [128, 4, 512], BF16)
    wi_s = sb("wi_s", [128, 4, 8])
    zaT = sb("zaT", [128, 8, 512], BF16)
    cvb = sb("cvb", [128, 4, 514])
    cvh = sb("cvh", [128, 8, 2])
    ycb = sb("ycb", [128, 4, 512])
    BvT = sb("BvT", [128, 8, 512], BF16)
    mT = sb("mT", [128, 8, 512], BF16)
    scoresS = [sb("scores%d" % i, [128, SEQ]) for i in range(2)]
    scores = scoresS[0]
    junk = sb("junk", [128, SEQ], mybir.dt.uint8)
    junk2 = sb("junk2", [128, SEQ], mybir.dt.uint8)
    bgate = scoresS[1][0:3, 0:D]
    negmS = [sb("negm%d" % i, [128, SEQ], BF16) for i in range(2)]
    rl = [sb("rl%d" % i, [128, 512]) for i in range(2)]
    bisS = [sb("bis%d" % i, [128, 8]) for i in range(2)]
    stpS = [sb("stp%d" % i, [128, NIT + 2]) for i in range(2)]
    PT = [sb("PT%d" % i, [128, 4, 128], BF16) for i in range(3)]
    qf = [sb("qf%d" % i, [128, 512]) for i in range(4)]
    qb = [sb("qb%d" % i, [128, 512], BF16) for i in range(4)]
    rt = [sb("rt%d" % i, [128, 256]) for i in range(4)]
    stat = sb("stat", [128, 8, 12])
    kvo = [sb("kvo%d" % i, [128, 512]) for i in range(4)]
    kio = [sb("kio%d" % i, [128, 64]) for i in range(4)]
    yst = rl
    tht = [sb("tht%d" % i, [128, 512]) for i in range(2)]
    tmpf = [sb("tmpf%d" % i, [128, 512]) for i in range(2)]
    xn = [tht[i][:].bitcast(BF16) for i in range(2)]
    ss = sb("ss", [128, 8])
    ms = sb("ms", [128, 8])
    rstd = sb("rstd", [128, 8])

    cstg = scores[:].rearrange("p (b c) -> p b c", c=256)
    GAh = sb("GAh", [128, 4, 512], BF16)
    GBh = sb("GBh", [128, 4, 512], BF16)
    yc2 = sb("yc2", [128, 512])

    ident = pbc[:, 0:128]
    I4 = pbc[:, 128:640]
    ones = pbc[:, 640:768]

    MM = [nc.alloc_psum_tensor("MM%d" % i, [128, 512], F32) for i in range(2)]
    TP = [nc.alloc_psum_tensor("TP%d" % i, [128, 1024], BF16) for i in range(2)]
    SP_ = [nc.alloc_psum_tensor("SS%d" % i, [128, 4, 128], F32) for i in range(2)]
    OTp = nc.alloc_psum_tensor("OTp", [128, 4, 128], F32)
    SMp = nc.alloc_psum_tensor("SMp", [128, 4, 128], F32)
    TPf = [TP[i][:].bitcast(F32) for i in range(2)]

    BMM = k.bufs("MM", 2)
    BTP = k.bufs("TP", 2)
    BSP = k.bufs("SP", 2)
    BOT = k.buf("OT")
    BSM = k.buf("SM")
    for b_ in BMM + BTP + BSP + [BOT, BSM]:
        b_.psum = True
    PB6 = [(SP_[0][:].rearrange("p h t -> p (h t)"), BSP[0]), (SP_[1][:].rearrange("p h t -> p (h t)"), BSP[1]),
           (OTp[:].rearrange("p h t -> p (h t)"), BOT), (SMp[:].rearrange("p h t -> p (h t)"), BSM)]
    TAILB = [(TPf[0], BTP[0]), (TPf[1], BTP[1])] + PB6 + [(MM[0][:], BMM[0]), (MM[1][:], BMM[1])]
    tailmode = {"on": False}
    TP4 = [(TP[0][:], BTP[0]), (TP[1][:], BTP[1]), (MM[0][:].bitcast(BF16), BMM[0]), (MM[1][:].bitcast(BF16), BMM[1])]
    rr = {"tb": 0, "tp4": 0, "xt": 0, "qf": 0, "st": 0, "pb6": 0, "mm": 0, "tp": 0, "sp": 0, "pt": 0, "rl": 0, "q": 0, "th": 0, "tf": 0, "yst": 0, "kvo": 0, "kio": 0}

    def rot(key, n):
        v = rr[key]
        rr[key] = (v + 1) % n
        return v

    Bc = k.buf("consts")
    Bmod = k.buf("mod")
    Bgbc = k.buf("gate_bc")
    Bxt = k.bufs("xt", 2)

    BhT = k.bufs("hT", 4)
    BQT = k.bufs("QT", 4)
    BKT = k.buf("KT")
    BV = k.buf("V")
    BkiT = k.buf("kiT")
    BqiT = k.bufs("qiT", 4)
    Bwi = k.buf("wi")
    BzaT = k.bufs("zaT", 8)
    Bcv = k.buf("cv")
    Bcvh = k.buf("cvh")
    Byc = k.buf("yc")
    BBv = k.bufs("Bv", 8)
    BmT = k.bufs("mT", 8)
    BscS = [k.bufs("sc%d_" % i, 4) for i in range(2)]
    Bsc = BscS[0]
    BnegmS = k.bufs("negm", 2)
    BbisS = k.bufs("bis", 2)
    Bjunk = k.buf("junk")
    Bjunk2 = k.buf("junk2")

    Brl = k.bufs("rl", 2)
    BPT = k.bufs("PT", 3)
    Bqf = k.bufs("qf", 4)
    Bqb = k.bufs("qb", 4)
    Brt = k.bufs("rt", 4)
    Bst = k.bufs("st", 8)
    Bkvo = k.bufs("kvo", 4)
    Bkio = k.bufs("kio", 4)
    Byst = Brl
    Btht = k.bufs("tht", 2)
    Bxn = Btht
    Btmpf = k.bufs("tmpf", 2)
    Bss = k.buf("ss")

    Bcst = Bsc
    BGA = k.buf("GA")
    BGB = k.buf("GB")
    Byc2 = k.buf("yc2")
    Bout = k.buf("dram_out")
    Bin = k.buf("dram_in")

    k.dma("sp", pf[:], pf_d, R=[Bin], W=[Bc], sb=Bc)
    k.dma("sp", pbc[:], pb_d, R=[Bin], W=[Bc], sb=Bc)
    k.dma("sp", sel[:], sel_d, R=[Bin], W=[Bc], sb=Bc)
    k.dma("sp", bgate, bgate_d, R=[Bin], W=BscS[1], sb=BscS[1][0])
    k.dma("sp", cT[:], cT_d, R=[Bin], W=[Bc], sb=Bc)
    issue_to(NRING)

    k.I("act", lambda e: e.activation(out=cth[:], in_=cT[:], func=AF.Tanh, scale=0.5), R=[Bc], W=[Bmod])
    k.I("dve", lambda e: e.scalar_tensor_tensor(out=cth[:], in0=cth[:], scalar=1.0, in1=cT[:], op0=ALU.add, op1=ALU.mult),
        R=[Bmod, Bc], W=[Bmod])
    k.I("dve", lambda e: e.tensor_scalar(scT[:], cth[:], 0.5, None, op0=ALU.mult), R=[Bmod], W=[Bmod])
    for j in range(6):
        wt, bw = next_block("ada%d" % j)
        if j < 4:
            i = rot("mm", 2)
            for c in range(4):
                for kc in range(8):
                    k.I("pe", lambda e, c=c, kc=kc: e.matmul(MM[i][:, c * 3:(c + 1) * 3], lhsT=wt[:, kc, c * 128:(c + 1) * 128],
                                                            rhs=scT[:, kc, :], start=(kc == 0), stop=(kc == 7)),
                        R=[bw, Bmod], W=[BMM[i]], sig=(kc == 7))
            k.I("dve", lambda e, j=j: e.tensor_tensor(
                out=modfm[:, 4 * j:4 * j + 4, :], in0=MM[i][:, 0:12].rearrange("p (c s) -> p c s", s=3),
                in1=pf[:, O_BADA + 4 * j:O_BADA + 4 * j + 4].unsqueeze(2).to_broadcast([128, 4, 3]), op=ALU.add),
                R=[BMM[i], Bc], W=[Bmod])
        else:
            i = rot("mm", 2)
            for kc in range(8):
                k.I("pe", lambda e, kc=kc: e.matmul(MM[i][0:3, :], lhsT=scT[:, kc, :], rhs=wt[:, kc, :],
                                                    start=(kc == 0), stop=(kc == 7)),
                    R=[bw, Bmod], W=[BMM[i]], sig=(kc == 7))
            c0 = (j - 4) * 512
            k.I("dve", lambda e, c0=c0: e.tensor_tensor(out=gate_tm[:, c0:c0 + 512], in0=MM[i][0:3, :],
                                                        in1=bgate[:, c0:c0 + 512], op=ALU.add),
                R=[BMM[i], Bc] + BscS[1], W=[Bmod])
    k.I("dve", lambda e: e.scalar_tensor_tensor(
        out=Amod[:], in0=modfm[:, 8:16, :], scalar=1.0,
        in1=pf[:, O_NG:O_NG + 8].unsqueeze(2).to_broadcast([128, 8, 3]), op0=ALU.add, op1=ALU.mult),
        R=[Bmod, Bc], W=[Bmod])
    k.I("dve", lambda e: e.tensor_scalar(gate_tm[:], gate_tm[:], 0.5, None, op0=ALU.mult), R=[Bmod], W=[Bmod])

    def rrobin(gens):
        gens = list(gens)
        if SEQUENTIAL_RR:
            for g_ in gens:
                for _ in g_:
                    pass
            return
        while gens:
            for g_ in list(gens):
                try:
                    next(g_)
                except StopIteration:
                    gens.remove(g_)

    def rope(eng, buf3, nh, half, cos, sin, Bb, rs):
        x1 = buf3[:, 0:nh, 0:half]
        x2 = buf3[:, 0:nh, half:2 * half]
        cb = cos.unsqueeze(1).to_broadcast([128, nh, half])
        sn = sin.unsqueeze(1).to_broadcast([128, nh, half])
        t = [rt[rs][:, i * 64:i * 64 + nh * half].rearrange("p (h d) -> p h d", d=half) for i in range(4)]
        k.I(eng, lambda e: e.tensor_tensor(out=t[0], in0=x1, in1=cb, op=ALU.mult), R=[Bb, Bc], W=[Brt[rs]])
        k.I(eng, lambda e: e.tensor_tensor(out=t[1], in0=x2, in1=sn, op=ALU.mult), R=[Bb, Bc], W=[Brt[rs]])
        k.I(eng, lambda e: e.tensor_tensor(out=t[2], in0=x2, in1=cb, op=ALU.mult), R=[Bb, Bc], W=[Brt[rs]])
        k.I(eng, lambda e: e.tensor_tensor(out=t[3], in0=x1, in1=sn, op=ALU.mult), R=[Bb, Bc], W=[Brt[rs]])
        k.I(eng, lambda e: e.tensor_tensor(out=x1, in0=t[0], in1=t[1], op=ALU.subtract), R=[Brt[rs]], W=[Bb])
        k.I(eng, lambda e: e.tensor_tensor(out=x2, in0=t[2], in1=t[3], op=ALU.add), R=[Brt[rs]], W=[Bb])

    def seq_pass(seq, p, NT, x_src, past, ptile0, outs, last):
        N = NT * 128
        nrow = 128 if seq < 2 else DEC_SEQ

        def gen_p1(t):
            xi = rot("xt", 2)
            sj = rot("st", 8)
            if nrow < 128:
                k.I("dve", lambda e: e.memset(xt[xi][:], 0.0), W=[Bxt[xi]])
            k.dma("sp", xt[xi][0:nrow, :], x_src(t), R=[Bin], W=[Bxt[xi]], sb=Bxt[xi])
            k.I("act", lambda e: e.activation(out=xn[xi][:], in_=xt[xi][:], func=AF.Square, accum_out=stat[:, sj, 0:1]),
                R=[Bxt[xi]], W=[Bxn[xi], Bst[sj]])
            yield
            k.I("dve", lambda e: e.tensor_scalar(stat[:, sj, 4:5], stat[:, sj, 0:1], 1.0 / D, EPS, op0=ALU.mult, op1=ALU.add),
                R=[Bst[sj]], W=[Bst[sj]])
            yield
            k.I("pool", lambda e: e.tensor_tensor(out=stat[:, sj, 8:9], in0=stat[:, sj, 4:5], in1=pf[:, O_NH:O_NH + 1], op=ALU.pow),
                R=[Bst[sj], Bc], W=[Bst[sj]])
            yield
            k.I("dve", lambda e: e.tensor_scalar(xn[xi][:], xt[xi][:], stat[:, sj, 8:9], None, op0=ALU.mult),
                R=[Bxt[xi], Bst[sj]], W=[Bxn[xi]])
            yield
            ti = rot("tp", 2)
            for kc in range(8):
                k.I("pe", lambda e, kc=kc: e.transpose(TP[ti][:, kc * 128:(kc + 1) * 128], xn[xi][:, kc * 128:(kc + 1) * 128], ident),
                    R=[Bxn[xi], Bc], W=[BTP[ti]], sig=(kc == 7))
            yield
            for kc in range(8):
                o = hT[:, kc, t * 128:(t + 1) * 128]
                i_ = TP[ti][:, kc * 128:(kc + 1) * 128]
                if t % 2 == 0:
                    k.I("act", lambda e: e.activation(out=o, in_=i_, func=AF.Identity,
                                                      scale=Amod[:, kc, seq:seq + 1], bias=modfm[:, kc, seq:seq + 1]),
                        R=[BTP[ti], Bmod], W=[BhT[t]])
                else:
                    k.I("dve", lambda e: e.tensor_scalar(o, i_, Amod[:, kc, seq:seq + 1], modfm[:, kc, seq:seq + 1],
                                                         op0=ALU.mult, op1=ALU.add),
                        R=[BTP[ti], Bmod], W=[BhT[t]])

        for t0 in range(0, NT, 2):
            rrobin([gen_p1(t) for t in range(t0, min(NT, t0 + 2))])

        if TRUNC == "p1":
            raise StopBuild()
        def tm_mm(wt, bw, t, ncols):
            i = rot("pb6", 4)
            o, Bo = PB6[i]
            for kc in range(8):
                k.I("pe", lambda e, kc=kc: e.matmul(o[:, 0:ncols], lhsT=hT[:, kc, t * 128:(t + 1) * 128], rhs=wt[:, kc, 0:ncols],
                                                    start=(kc == 0), stop=(kc == 7)),
                    R=[bw, BhT[t]], W=[Bo], sig=(kc == 7))
            return o, Bo

        def norm_heads(o, Bo, nh, gofs, qi_, sj):
            for h in range(nh):
                k.I("act", lambda e, h=h: e.activation(out=junk[:, h * 128:(h + 1) * 128], in_=o[:, h * 128:(h + 1) * 128],
                                                       func=AF.Square, accum_out=stat[:, sj, h:h + 1]),
                    R=[Bo], W=[Bjunk, Bst[sj]])
            yield
            k.I("dve", lambda e: e.tensor_scalar(stat[:, sj, 4:4 + nh], stat[:, sj, 0:nh], 1.0 / 128, EPS, op0=ALU.mult, op1=ALU.add),
                R=[Bst[sj]], W=[Bst[sj]])
            yield
            k.I("pool", lambda e: e.tensor_tensor(out=stat[:, sj, 8:8 + nh], in0=stat[:, sj, 4:4 + nh], in1=pf[:, O_NH:O_NH + nh], op=ALU.pow),
                R=[Bst[sj], Bc], W=[Bst[sj]])
            yield
            for h in range(nh):
                k.I("dve", lambda e, h=h: e.scalar_tensor_tensor(out=qf[qi_][:, h * 128:(h + 1) * 128], in0=o[:, h * 128:(h + 1) * 128],
                                                                 scalar=stat[:, sj, 8 + h:9 + h], in1=pf[:, gofs:gofs + 128],
                                                                 op0=ALU.mult, op1=ALU.mult),
                    R=[Bo, Bst[sj], Bc], W=[Bqf[qi_]])

        def cs16(t):
            o = (ptile0 + t) * 16
            return pf[:, O_COS16 + o:O_COS16 + o + 16], pf[:, O_SIN16 + o:O_SIN16 + o + 16]

        def cs8(t):
            o = (ptile0 + t) * 8
            return pf[:, O_COS8 + o:O_COS8 + o + 8], pf[:, O_SIN8 + o:O_SIN8 + o + 8]

        def kbof(t):
            return ((past // 128) + p * 4 + t) if seq < 2 else past // 128

        def gen_q(g, wt, bw, t):
            o, Bo = tm_mm(wt, bw, t, 512)
            qi_ = rot("qf", 4)
            sj = rot("st", 8)
            yield
            yield from norm_heads(o, Bo, 4, O_GQ, qi_, sj)
            c16, s16 = cs16(t)
            rope("dve", qf[qi_][:].rearrange("p (h d) -> p h d", d=128), 4, 16, c16, s16, Bqf[qi_], qi_)
            yield
            k.I("act", lambda e: e.activation(out=qb[qi_][:], in_=qf[qi_][:], func=AF.Copy), R=[Bqf[qi_]], W=[Bqb[qi_]])
            yield
            tpo, Btp = TP4[rot("tp4", 4)]
            for h in range(4):
                k.I("pe", lambda e, h=h: e.transpose(tpo[:, h * 128:(h + 1) * 128], qb[qi_][:, h * 128:(h + 1) * 128], ident),
                    R=[Bqb[qi_], Bc], W=[Btp], sig=(h == 3))
            yield
            k.I("act", lambda e: e.activation(out=QT[:, 4 * g:4 * g + 4, t * 128:(t + 1) * 128],
                                              in_=tpo[:, 0:512].rearrange("p (h t) -> p h t", t=128), func=AF.Copy),
                R=[Btp], W=[BQT[t]])

        def gen_kv(wt, bw, t):
            o, Bo = tm_mm(wt, bw, t, 512)
            qi_ = rot("qf", 4)
            sj = rot("st", 8)
            ko = rot("kvo", 4)
            kb = kbof(t)
            yield
            k.I("act", lambda e: e.activation(out=kvo[ko][:, 256:512], in_=o[:, 256:512], func=AF.Copy), R=[Bo], W=[Bkvo[ko]])
            k.I("dve", lambda e: e.tensor_copy(Vb[:, kb, :], kvo[ko][:, 256:512]), R=[Bkvo[ko]], W=[BV])
            yield from norm_heads(o, Bo, 2, O_GK, qi_, sj)
            c16, s16 = cs16(t)
            rope("dve", qf[qi_][:, 0:256].rearrange("p (h d) -> p h d", d=128), 2, 16, c16, s16, Bqf[qi_], qi_)
            yield
            k.I("act", lambda e: e.activation(out=kvo[ko][:, 0:256], in_=qf[qi_][:, 0:256], func=AF.Copy), R=[Bqf[qi_]], W=[Bkvo[ko]])
            k.I("dve", lambda e: e.tensor_copy(qb[qi_][:, 0:256], qf[qi_][:, 0:256]), R=[Bqf[qi_]], W=[Bqb[qi_]])
            yield
            tpo, Btp = TP4[rot("tp4", 4)]
            for h in range(2):
                k.I("pe", lambda e, h=h: e.transpose(tpo[:, h * 128:(h + 1) * 128], qb[qi_][:, h * 128:(h + 1) * 128], ident),
                    R=[Bqb[qi_], Bc], W=[Btp], sig=(h == 1))
            ok, ov, oki = outs["kv"](t)
            k.dma("act", ok, kvo[ko][0:nrow, 0:256], R=[Bkvo[ko]], sb=Bkvo[ko])
            k.dma("act", ov, kvo[ko][0:nrow, 256:512], R=[Bkvo[ko]], sb=Bkvo[ko])
            yield
            k.I("act", lambda e: e.activation(out=KT[:, 0:2, kb * 128:(kb + 1) * 128],
                                              in_=tpo[:, 0:256].rearrange("p (h t) -> p h t", t=128), func=AF.Copy),
                R=[Btp], W=[BKT])

        def gen_qi(wt, bw, t):
            o, Bo = tm_mm(wt, bw, t, 512)
            qi_ = rot("qf", 4)
            yield
            k.I("act", lambda e: e.activation(out=qf[qi_][:], in_=o[:], func=AF.Copy), R=[Bo], W=[Bqf[qi_]])
            yield
            c8, s8 = cs8(t)
            rope("dve", qf[qi_][:].rearrange("p (h d) -> p h d", d=64), 8, 8, c8, s8, Bqf[qi_], qi_)
            yield
            k.I("act", lambda e: e.activation(out=qb[qi_][:], in_=qf[qi_][:], func=AF.Copy), R=[Bqf[qi_]], W=[Bqb[qi_]])
            yield
            tpo, Btp = TP4[rot("tp4", 4)]
            for h in range(4):
                k.I("pe", lambda e, h=h: e.transpose(tpo[:, h * 128:(h + 1) * 128], qb[qi_][:, h * 128:(h + 1) * 128], ident),
                    R=[Bqb[qi_], Bc], W=[Btp], sig=(h == 3))
            yield
            k.I("act", lambda e: e.activation(out=qiT[:, 0:4, t * 128:(t + 1) * 128],
                                              in_=tpo[:, 0:512].rearrange("p (h t) -> p h t", t=128), func=AF.Copy),
                R=[Btp], W=[BqiT[t]])

        def gen_kiwi(wt, bw, t):
            o, Bo = tm_mm(wt, bw, t, 72)
            qi_ = rot("qf", 4)
            ko = rot("kio", 4)
            kb = kbof(t)
            yield
            k.I("act", lambda e: e.activation(out=qf[qi_][:, 0:64], in_=o[:, 0:64], func=AF.Copy), R=[Bo], W=[Bqf[qi_]])
            k.I("act", lambda e: e.activation(out=wi_s[:, t, :], in_=o[:, 64:72], func=AF.Identity, scale=WSC), R=[Bo], W=[Bwi])
            yield
            c8, s8 = cs8(t)
            rope("dve", qf[qi_][:, 0:64].rearrange("p (h d) -> p h d", d=64), 1, 8, c8, s8, Bqf[qi_], qi_)
            yield
            k.I("act", lambda e: e.activation(out=kio[ko][:], in_=qf[qi_][:, 0:64], func=AF.Copy), R=[Bqf[qi_]], W=[Bkio[ko]])
            k.I("dve", lambda e: e.tensor_copy(qb[qi_][:, 0:64], qf[qi_][:, 0:64]), R=[Bqf[qi_]], W=[Bqb[qi_]])
            k.I("dve", lambda e: e.tensor_copy(qb[qi_][:, 64:128], qf[qi_][:, 0:64]), R=[Bqf[qi_]], W=[Bqb[qi_]])
            yield
            tpo, Btp = TP4[rot("tp4", 4)]
            k.I("pe", lambda e: e.transpose(tpo[:, 0:128], qb[qi_][:, 0:128], ident), R=[Bqb[qi_], Bc], W=[Btp])
            ok, ov, oki = outs["kv"](t)
            k.dma("act", oki, kio[ko][0:nrow, :], R=[Bkio[ko]], sb=Bkio[ko])
            yield
            k.I("act", lambda e: e.activation(out=kiT[:, kb * 128:(kb + 1) * 128], in_=tpo[:, 0:128], func=AF.Copy),
                R=[Btp], W=[BkiT])

        for g in range(2):
            wt, bw = next_block("q%d" % g)
            rrobin([gen_q(g, wt, bw, t) for t in range(NT)])
        if TRUNC == "q":
            raise StopBuild()
        wt, bw = next_block("kv")
        rrobin([gen_kv(wt, bw, t) for t in range(NT)])
        if TRUNC == "kv":
            raise StopBuild()
        wt, bw = next_block("qi")
        rrobin([gen_qi(wt, bw, t) for t in range(NT)])
        if TRUNC == "qi":
            raise StopBuild()
        wt, bw = next_block("kiwi")
        rrobin([gen_kiwi(wt, bw, t) for t in range(NT)])

        if TRUNC == "p2":
            raise StopBuild()
        def pick_bank():
            if tailmode["on"]:
                return TAILB[rot("tb", 8)]
            i = rot("tp", 2)
            return TPf[i], BTP[i]

        def fm_mm(wt, bw, c):
            o, Bo_ = pick_bank()
            for kc in range(8):
                k.I("pe", lambda e, kc=kc: e.matmul(o[:, 0:N], lhsT=wt[:, kc, c * 128:(c + 1) * 128], rhs=hT[:, kc, 0:N],
                                                    start=(kc == 0), stop=(kc == 7)),
                    R=[bw] + BhT[0:NT], W=[Bo_], sig=(kc == 7))
            return o, Bo_

        def tail_mm(wt, bw, c, src, Bsrc):
            o, Bo_ = pick_bank()
            for kc in range(8):
                k.I("pe", lambda e, kc=kc: e.matmul(o[:, 0:N], lhsT=wt[:, kc, c * 128:(c + 1) * 128], rhs=src[:, kc, 0:N],
                                                    start=(kc == 0), stop=(kc == 7)),
                    R=[bw] + Bsrc, W=[Bo_], sig=(kc == 7))
            return o, Bo_

        def gen_za():
            for hf in range(2):
                wt, bw = next_block("za%d" % hf)
                for c in range(4):
                    o, Bo = fm_mm(wt, bw, c)
                    th = rot("th", 2)
                    ch = hf * 4 + c
                    k.I("act", lambda e: e.activation(out=tht[th][:, 0:N], in_=o[:, 0:N], func=AF.Tanh, scale=0.5),
                        R=[Bo], W=[Btht[th]])
                    k.I("dve", lambda e: e.scalar_tensor_tensor(out=zaT[:, ch, 0:N], in0=tht[th][:, 0:N], scalar=1.0, in1=o[:, 0:N],
                                                                op0=ALU.add, op1=ALU.mult),
                        R=[Btht[th], Bo], W=[BzaT[ch]])
                    yield "y"

        def gen_Y():
            for hf in range(2):
                wt, bw = next_block("u%d" % hf)
                k.I("pool", lambda e: e.tensor_copy(cvb[:, :, 0:2], cvh[:, hf * 4:hf * 4 + 4, :]), R=[Bcvh], W=[Bcv])
                for c in range(4):
                    o, Bo = fm_mm(wt, bw, c)
                    k.I("act", lambda e: e.activation(out=cvb[:, c, 2:2 + N], in_=o[:, 0:N], func=AF.Copy), R=[Bo], W=[Bcv])
                    yield "y"
                wt, bw = next_block("cg%d" % hf)
                for c in range(4):
                    o, Bo = fm_mm(wt, bw, c)
                    k.I("dve", lambda e: e.tensor_tensor(out=cvb[:, c, 2:2 + N], in0=o[:, 0:N], in1=cvb[:, c, 2:2 + N], op=ALU.mult),
                        R=[Bo, Bcv], W=[Bcv])
                    yield "y"
                nv = N if seq < 2 else DEC_SEQ
                k.I("pool", lambda e: e.tensor_copy(cvh[:, hf * 4:hf * 4 + 4, :], cvb[:, :, nv:nv + 2]), R=[Bcv], W=[Bcvh])
                for c in range(4):
                    ch = hf * 4 + c
                    w0 = pf[:, O_CW + 0 * 8 + ch:O_CW + 0 * 8 + ch + 1]
                    w1 = pf[:, O_CW + 1 * 8 + ch:O_CW + 1 * 8 + ch + 1]
                    w2 = pf[:, O_CW + 2 * 8 + ch:O_CW + 2 * 8 + ch + 1]
                    k.I("pool", lambda e: e.tensor_scalar(ycb[:, c, 0:N], cvb[:, c, 0:N], w0, 0.0, op0=ALU.mult, op1=ALU.add), R=[Bcv, Bc], W=[Byc])
                    k.I("pool", lambda e: e.tensor_scalar(yc2[:, 0:N], cvb[:, c, 1:1 + N], w1, 0.0, op0=ALU.mult, op1=ALU.add), R=[Bcv, Bc], W=[Byc2])
                    k.I("pool", lambda e: e.tensor_tensor(out=ycb[:, c, 0:N], in0=ycb[:, c, 0:N], in1=yc2[:, 0:N], op=ALU.add), R=[Byc, Byc2], W=[Byc])
                    k.I("pool", lambda e: e.tensor_scalar(yc2[:, 0:N], cvb[:, c, 2:2 + N], w2, 0.0, op0=ALU.mult, op1=ALU.add), R=[Bcv, Bc], W=[Byc2])
                    k.I("pool", lambda e: e.tensor_tensor(out=ycb[:, c, 0:N], in0=ycb[:, c, 0:N], in1=yc2[:, 0:N], op=ALU.add), R=[Byc, Byc2], W=[Byc])
                    yield "y"
                wt, bw = next_block("bg%d" % hf)
                for c in range(4):
                    o, Bo = fm_mm(wt, bw, c)
                    k.I("dve", lambda e: e.scalar_tensor_tensor(out=ycb[:, c, 0:N], in0=o[:, 0:N], scalar=0.5, in1=ycb[:, c, 0:N],
                                                                op0=ALU.mult, op1=ALU.mult),
                        R=[Bo, Byc], W=[Byc])
                    yield "y"
                wt, bw = next_block("zb%d" % hf)
                for c in range(4):
                    o, Bo = fm_mm(wt, bw, c)
                    th = rot("th", 2)
                    ch = hf * 4 + c
                    k.I("act", lambda e: e.activation(out=tht[th][:, 0:N], in_=o[:, 0:N], func=AF.Tanh, scale=0.5),
                        R=[Bo], W=[Btht[th]])
                    k.I("dve", lambda e: e.scalar_tensor_tensor(out=tht[th][:, 0:N], in0=tht[th][:, 0:N], scalar=1.0, in1=o[:, 0:N],
                                                                op0=ALU.add, op1=ALU.mult),
                        R=[Btht[th], Bo], W=[Btht[th]])
                    k.I("pool", lambda e: e.tensor_tensor(out=BvT[:, ch, 0:N], in0=tht[th][:, 0:N], in1=ycb[:, c, 0:N], op=ALU.mult),
                        R=[Btht[th], Byc], W=[BBv[ch]])
                    yield "y"
            for hf in range(2):
                wt, bw = next_block("gb%d" % hf)
                for c in range(4):
                    o, Bo = fm_mm(wt, bw, c)
                    k.I("act", lambda e: e.activation(out=GBh[:, c, 0:N], in_=o[:, 0:N], func=AF.Tanh, scale=0.5), R=[Bo], W=[BGB])
                    yield "y"
                wt, bw = next_block("pb%d" % hf)
                for c in range(4):
                    ch = hf * 4 + c
                    o, Bo = tail_mm(wt, bw, c, BvT, BBv)
                    k.I("dve", lambda e: e.scalar_tensor_tensor(out=mT[:, ch, 0:N], in0=GBh[:, c, 0:N], scalar=1.0, in1=o[:, 0:N],
                                                                op0=ALU.add, op1=ALU.mult),
                        R=[BGB, Bo], W=[BmT[ch]])
                    yield "y"

        def gen_X(tiles, sx):
            scores, negm, bis, stp = scoresS[sx], negmS[sx], bisS[sx], stpS[sx]
            Bsc, Bnegm, Bbis = BscS[sx], BnegmS[sx], BbisS[sx]
            use_act = (sx == 1)
            for t in tiles:
                if seq < 2:
                    L = (p * 4 + t + 1) * 128
                    Lmin = L - 128
                    bisect = L > 256
                else:
                    L = past + 128
                    Lmin = past
                    bisect = True
                nkb = L // 128
                nb5 = (L + 511) // 512
                for h in range(8):
                    for b5 in range(nb5):
                        c0 = b5 * 512
                        n = min(512, L - c0)
                        i = rot("mm", 2)
                        pb_ = (h % 2) * 64
                        k.I("pe", lambda e: e.matmul(MM[i][:, 0:n], lhsT=qiT[pb_:pb_ + 64, h // 2, t * 128:(t + 1) * 128],
                                                     rhs=kiT[pb_:pb_ + 64, c0:c0 + n], start=True, stop=True),
                            R=[BqiT[t], BkiT], W=[BMM[i]])
                        r = rot("rl", 2)
                        k.I("act", lambda e: e.activation(out=rl[r][:, 0:n], in_=MM[i][:, 0:n], func=AF.Relu), R=[BMM[i]], W=[Brl[r]])
                        if h == 0:
                            k.I("dve", lambda e: e.tensor_scalar(scores[:, c0:c0 + n], rl[r][:, 0:n], wi_s[:, t, h:h + 1], None, op0=ALU.mult),
                                R=[Brl[r], Bwi], W=[Bsc[b5]])
                        else:
                            k.I("dve", lambda e: e.scalar_tensor_tensor(out=scores[:, c0:c0 + n], in0=rl[r][:, 0:n], scalar=wi_s[:, t, h:h + 1],
                                                                        in1=scores[:, c0:c0 + n], op0=ALU.mult, op1=ALU.add),
                                R=[Brl[r], Bwi, Bsc[b5]], W=[Bsc[b5]])
                        yield "idx"
                if seq < 2:
                    k.I("dve", lambda e: e.memset(scores[0:64, L - 64:L], NEGBIG), R=Bsc, W=Bsc)
                else:
                    k.I("dve", lambda e: e.memset(scores[:, past + DEC_SEQ:L], NEGBIG), R=Bsc, W=Bsc)
                if bisect:
                    k.I("dve", lambda e: e.tensor_reduce(out=bis[:, 0:1], in_=scores[:, 0:L], axis=AX.X, op=ALU.max), R=Bsc, W=[Bbis])
                    k.I("dve", lambda e: e.tensor_reduce(out=bis[:, 1:2], in_=scores[:, 0:Lmin], axis=AX.X, op=ALU.min), R=Bsc, W=[Bbis])
                    k.I("dve", lambda e: e.tensor_tensor(out=bis[:, 2:3], in0=bis[:, 0:1], in1=bis[:, 1:2], op=ALU.subtract), R=[Bbis], W=[Bbis])
                    k.I("dve", lambda e: e.tensor_scalar(stp[:], pf[:, O_POW2:O_POW2 + NIT + 2], bis[:, 2:3], None, op0=ALU.mult),
                        R=[Bbis, Bc], W=[Bbis])
                    k.I("dve", lambda e: e.tensor_tensor(out=bis[:, 3:4], in0=bis[:, 1:2], in1=stp[:, 2:3], op=ALU.add), R=[Bbis], W=[Bbis])
                    yield "bis"
                    if not use_act:
                        for it in range(1, NIT + 1):
                            k.I("dve", lambda e: e.tensor_scalar(junk[:, 0:L], scores[:, 0:L], bis[:, 3:4], 0.0, op0=ALU.is_ge, op1=ALU.add,
                                                                 accum_out=bis[:, 4:5]),
                                R=Bsc + [Bbis], W=[Bjunk, Bbis])
                            k.I("dve", lambda e: e.tensor_scalar(bis[:, 5:6], bis[:, 4:5], TOPK, 0.5, op0=ALU.is_ge, op1=ALU.subtract),
                                R=[Bbis], W=[Bbis])
                            if it < NIT:
                                k.I("dve", lambda e: e.scalar_tensor_tensor(out=bis[:, 3:4], in0=bis[:, 5:6], scalar=stp[:, it + 1:it + 2],
                                                                            in1=bis[:, 3:4], op0=ALU.mult, op1=ALU.add),
                                    R=[Bbis], W=[Bbis])
                            else:
                                k.I("dve", lambda e: e.tensor_scalar(bis[:, 5:6], bis[:, 5:6], 0.5, None, op0=ALU.subtract), R=[Bbis], W=[Bbis])
                                k.I("dve", lambda e: e.scalar_tensor_tensor(out=bis[:, 6:7], in0=bis[:, 5:6], scalar=stp[:, NIT + 1:NIT + 2],
                                                                            in1=bis[:, 3:4], op0=ALU.mult, op1=ALU.add),
                                    R=[Bbis], W=[Bbis])
                            yield "bis"
                    else:
                        k.I("dve", lambda e: e.tensor_scalar(bis[:, 7:8], bis[:, 3:4], -1.0, None, op0=ALU.mult), R=[Bbis], W=[Bbis])
                        for it in range(1, NIT + 1):
                            k.I("act", lambda e: e.activation(out=junk2[:, 0:L], in_=scores[:, 0:L], func=AF.Sign, bias=bis[:, 7:8], scale=1.0,
                                                              accum_out=bis[:, 4:5]),
                                R=Bsc + [Bbis], W=[Bjunk2, Bbis])
                            k.I("dve", lambda e: e.tensor_scalar(bis[:, 5:6], bis[:, 4:5], float(2 * TOPK - L), 0.5, op0=ALU.is_lt, op1=ALU.subtract),
                                R=[Bbis], W=[Bbis])
                            if it < NIT:
                                k.I("dve", lambda e: e.scalar_tensor_tensor(out=bis[:, 7:8], in0=bis[:, 5:6], scalar=stp[:, it + 1:it + 2],
                                                                            in1=bis[:, 7:8], op0=ALU.mult, op1=ALU.add),
                                    R=[Bbis], W=[Bbis])
                            else:
                                k.I("dve", lambda e: e.tensor_scalar(bis[:, 5:6], bis[:, 5:6], 0.5, None, op0=ALU.add), R=[Bbis], W=[Bbis])
                                k.I("dve", lambda e: e.scalar_tensor_tensor(out=bis[:, 6:7], in0=bis[:, 5:6], scalar=stp[:, NIT + 1:NIT + 2],
                                                                            in1=bis[:, 7:8], op0=ALU.mult, op1=ALU.add),
                                    R=[Bbis], W=[Bbis])
                                k.I("dve", lambda e: e.tensor_scalar(bis[:, 6:7], bis[:, 6:7], -1.0, None, op0=ALU.mult), R=[Bbis], W=[Bbis])
                            yield "bis"
                    k.I("dve", lambda e: e.tensor_scalar(negm[:, 0:L], scores[:, 0:L], bis[:, 6:7], NEG, op0=ALU.is_lt, op1=ALU.mult),
                        R=Bsc + [Bbis], W=[Bnegm])
                else:
                    k.I("dve", lambda e: e.tensor_scalar(negm[:, 0:L], scores[:, 0:L], -1.0e29, NEG, op0=ALU.is_lt, op1=ALU.mult),
                        R=Bsc, W=[Bnegm])
                yield "bis"
                yield "want_att"
                for g in range(2):
                    for kb in range(nkb):
                        si = rot("sp", 2)
                        k.I("pe", lambda e: e.matmul(SP_[si][:], lhsT=KT[:, g, kb * 128:(kb + 1) * 128],
                                                     rhs=QT[:, 4 * g:4 * g + 4, t * 128:(t + 1) * 128], start=True, stop=False),
                            R=[BKT, BQT[t]], W=[BSP[si]], sig=False)
                        k.I("pe", lambda e: e.matmul(SP_[si][:], lhsT=negm[:, kb * 128:(kb + 1) * 128],
                                                     rhs=I4.rearrange("p (h t) -> p h t", t=128), start=False, stop=True),
                            R=[Bnegm, Bc], W=[BSP[si]])
                        pi = rot("pt", 3)
                        k.I("act", lambda e: e.activation(out=PT[pi][:], in_=SP_[si][:], func=AF.Exp, scale=SCALE), R=[BSP[si]], W=[BPT[pi]])
                        k.I("pe", lambda e: e.matmul(OTp[:], lhsT=Vb[:, kb, g * 128:(g + 1) * 128], rhs=PT[pi][:],
                                                     start=(kb == 0), stop=(kb == nkb - 1)),
                            R=[BV, BPT[pi]], W=[BOT], sig=False)
                        k.I("pe", lambda e: e.matmul(SMp[:], lhsT=ones, rhs=PT[pi][:], start=(kb == 0), stop=(kb == nkb - 1)),
                            R=[Bc, BPT[pi]], W=[BSM], sig=True)
                        yield "att"
                    tf = rot("tf", 2)
                    t3 = tmpf[tf][:].rearrange("p (h t) -> p h t", t=128)
                    k.I("dve", lambda e: e.reciprocal(t3, SMp[:]), R=[BSM], W=[Btmpf[tf]])
                    k.I("dve", lambda e: e.tensor_tensor(out=t3, in0=OTp[:], in1=t3, op=ALU.mult), R=[BOT, Btmpf[tf]], W=[Btmpf[tf]])
                    k.I("dve", lambda e: e.scalar_tensor_tensor(out=zaT[:, 4 * g:4 * g + 4, t * 128:(t + 1) * 128], in0=t3, scalar=0.5,
                                                                in1=zaT[:, 4 * g:4 * g + 4, t * 128:(t + 1) * 128], op0=ALU.mult, op1=ALU.mult),
                        R=[Btmpf[tf]] + BzaT[4 * g:4 * g + 4], W=BzaT[4 * g:4 * g + 4])
                    yield "bis"
                yield "att_done"

        gz = gen_za()
        gy_main = gen_Y()

        def _chain():
            for x_ in gz:
                yield x_
            for x_ in gy_main:
                yield x_
        gy = _chain()
        ny_left = 64
        if seq < 2:
            nbt = sum(1 for t in range(NT) if (p * 4 + t + 1) * 128 > 256)
        else:
            nbt = 1
        nbis_left = nbt * (NIT + 2) + NT * 3
        st_y = {"done": False, "ny": ny_left, "nb": nbis_left}

        def y_steps():
            kq = 0 if st_y["done"] else max(1, -(-st_y["ny"] // max(1, st_y["nb"])))
            st_y["nb"] -= 1
            for _ in range(kq):
                try:
                    next(gy)
                    st_y["ny"] -= 1
                except StopIteration:
                    st_y["done"] = True
                    break

        streams = [gen_X([t for t in range(NT) if t % 2 == sx], sx) for sx in range(2)]
        want = [False, False]
        alive = [True, True]
        lock = None
        while any(alive):
            for sx in range(2):
                if not alive[sx]:
                    continue
                if want[sx]:
                    if lock is not None and lock != sx:
                        continue
                    lock = sx
                    want[sx] = False
                tag = next(streams[sx], None)
                if tag is None:
                    alive[sx] = False
                    if lock == sx:
                        lock = None
                elif tag == "want_att":
                    want[sx] = True
                    for _ in gz:
                        st_y["ny"] -= 1
                elif tag == "att_done":
                    if lock == sx:
                        lock = None
                elif tag == "bis":
                    y_steps()
        for _ in gy:
            pass

        if TRUNC == "xy":
            raise StopBuild()
        tailmode["on"] = True
        for hf in range(2):
            wt, bw = next_block("ga%d" % hf)
            for c in range(4):
                o, Bo = fm_mm(wt, bw, c)
                k.I("act", lambda e: e.activation(out=GAh[:, c, 0:N], in_=o[:, 0:N], func=AF.Tanh, scale=0.5), R=[Bo], W=[BGA])
            wt, bw = next_block("pa%d" % hf)
            for c in range(4):
                ch = hf * 4 + c
                o, Bo = tail_mm(wt, bw, c, zaT, BzaT)
                tf = rot("tf", 2)
                k.I("dve", lambda e: e.scalar_tensor_tensor(out=tmpf[tf][:, 0:N], in0=GAh[:, c, 0:N], scalar=1.0, in1=o[:, 0:N],
                                                            op0=ALU.add, op1=ALU.mult),
                    R=[BGA, Bo], W=[Btmpf[tf]])
                k.I("dve", lambda e: e.tensor_tensor(out=mT[:, ch, 0:N], in0=tmpf[tf][:, 0:N], in1=mT[:, ch, 0:N], op=ALU.add),
                    R=[Btmpf[tf], BmT[ch]], W=[BmT[ch]])
        for hf in range(2):
            wt, bw = next_block("out%d" % hf)
            for t in range(NT):
                xi = rot("xt", 2)
                k.dma("sp", xt[xi][0:nrow, 0:512], x_src(t)[:, hf * 512:(hf + 1) * 512], R=[Bin], W=[Bxt[xi]], sb=Bxt[xi])
                ob, Bob = TAILB[rot("tb", 8)]
                for kc in range(8):
                    k.I("pe", lambda e, kc=kc: e.matmul(ob[:, 0:512], lhsT=mT[:, kc, t * 128:(t + 1) * 128], rhs=wt[:, kc, :],
                                                        start=(kc == 0), stop=(kc == 7)),
                        R=[bw] + BmT, W=[Bob], sig=(kc == 7))
                ys = rot("yst", 2)
                k.I("dve", lambda e: e.tensor_tensor(out=yst[ys][:], in0=ob[:, 0:512], in1=gate_bc[:, hf * 512:(hf + 1) * 512], op=ALU.mult),
                    R=[Bob, Bgbc], W=[Byst[ys]])
                k.I("dve", lambda e: e.tensor_tensor(out=yst[ys][0:nrow, :], in0=yst[ys][0:nrow, :], in1=xt[xi][0:nrow, 0:512], op=ALU.add),
                    R=[Byst[ys], Bxt[xi]], W=[Byst[ys]])
                k.dma("act", outs["y"](t)[:, hf * 512:(hf + 1) * 512], yst[ys][0:nrow, :], R=[Byst[ys]], sb=Byst[ys])
        tailmode["on"] = False
        if last:
            k.dma("sp", outs["conv"], cvh[:], R=[Bcvh], sb=Bcvh)

    def seq_setup(seq):
        for hf in range(2):
            i = rot("mm", 2)
            k.I("pe", lambda e: e.matmul(MM[i][:], lhsT=sel[:, seq * 128:(seq + 1) * 128], rhs=gate_tm[:, hf * 512:(hf + 1) * 512],
                                         start=True, stop=True),
                R=[Bc, Bmod], W=[BMM[i]])
            k.I("act", lambda e: e.activation(out=gate_bc[:, hf * 512:(hf + 1) * 512], in_=MM[i][:], func=AF.Copy), R=[BMM[i]], W=[Bgbc])

    def _run_all():
        for seq in range(2):
            seq_setup(seq)
            k.I("dve", lambda e: e.memset(cvh[:], 0.0), W=[Bcvh])
            for p in range(4):
                outs = {
                    "kv": lambda t, seq=seq, p=p: (k_p[seq, (p * 4 + t) * 128:(p * 4 + t + 1) * 128, :],
                                                    v_p[seq, (p * 4 + t) * 128:(p * 4 + t + 1) * 128, :],
                                                    ki_p[seq, (p * 4 + t) * 128:(p * 4 + t + 1) * 128, :]),
                    "y": lambda t, seq=seq, p=p: y_p[seq, (p * 4 + t) * 128:(p * 4 + t + 1) * 128, :],
                    "conv": conv_p[seq],
                }
                seq_pass(seq, p, 4, lambda t, seq=seq, p=p: xp[seq, (p * 4 + t) * 128:(p * 4 + t + 1) * 128, :],
                         0, p * 4, outs, p == 3)

        seq_setup(2)
        k.dma("sp", cvh[:], sconv, R=[Bin], W=[Bcvh], sb=Bcvh)
        for half in range(2):
            k.dma("sp", cstg[:], ck.rearrange("(b p) c -> p b c", p=128), R=[Bin], W=Bcst, sb=Bcst[0]) if half == 0 else None
            if half == 0:
                for b in range(8):
                    qi_ = rot("qf", 4)
                    k.I("dve", lambda e, b=b: e.tensor_copy(qb[qi_][:, 0:256], cstg[:, b, :]), R=Bcst, W=[Bqb[qi_]])
                    ti = rot("tp", 2)
                    for h in range(2):
                        k.I("pe", lambda e, h=h: e.transpose(TP[ti][:, h * 128:(h + 1) * 128], qb[qi_][:, h * 128:(h + 1) * 128], ident),
                            R=[Bqb[qi_], Bc], W=[BTP[ti]], sig=(h == 1))
                    k.I("act", lambda e, b=b: e.activation(out=KT[:, 0:2, b * 128:(b + 1) * 128],
                                                           in_=TP[ti][:, 0:256].rearrange("p (h t) -> p h t", t=128), func=AF.Copy),
                        R=[BTP[ti]], W=[BKT])
            else:
                k.dma("sp", cstg[:], cvv.rearrange("(b p) c -> p b c", p=128), R=[Bin], W=Bcst, sb=Bcst[0])
                k.I("dve", lambda e: e.tensor_copy(Vb[:, 0:8, :], cstg[:]), R=Bcst, W=[BV])
        k.dma("sp", cstg[:, :, 0:64], cik.rearrange("(b p) c -> p b c", p=128), R=[Bin], W=Bcst, sb=Bcst[0])
        for b in range(8):
            qi_ = rot("qf", 4)
            k.I("dve", lambda e, b=b: e.tensor_copy(qb[qi_][:, 0:64], cstg[:, b, 0:64]), R=Bcst, W=[Bqb[qi_]])
            k.I("dve", lambda e, b=b: e.tensor_copy(qb[qi_][:, 64:128], cstg[:, b, 0:64]), R=Bcst, W=[Bqb[qi_]])
            ti = rot("tp", 2)
            k.I("pe", lambda e: e.transpose(TP[ti][:, 0:128], qb[qi_][:, 0:128], ident), R=[Bqb[qi_], Bc], W=[BTP[ti]])
            k.I("act", lambda e, b=b: e.activation(out=kiT[:, b * 128:(b + 1) * 128], in_=TP[ti][:, 0:128], func=AF.Copy),
                R=[BTP[ti]], W=[BkiT])
        outs = {
            "kv": lambda t: (k_s[:, :], v_s[:, :], ki_s[:, :]),
            "y": lambda t: y_s[:, :],
            "conv": conv_s,
        }
        seq_pass(2, 0, 1, lambda t: xs[:, :], PAST, 16, outs, True)


    try:
        _run_all()
    except StopBuild:
        pass
    k.E['pe'].pending = False
    k.finish()
    return nc


_PROG = {}


def _rope_tables():
    pos = np.zeros((NPT, 128), np.float32)
    for t in range(16):
        pos[t] = np.arange(t * 128, (t + 1) * 128, dtype=np.float32)
    pos[16] = np.arange(PAST, PAST + 128, dtype=np.float32)
    out = {}
    for rot_, name in ((32, "16"), (16, "8")):
        half = rot_ // 2
        inv = (np.float32(500000.0) ** (np.float32(-2.0) * np.arange(half, dtype=np.float32) / np.float32(rot_))).astype(np.float32)
        ang = (pos[:, :, None] * inv[None, None, :]).astype(np.float32)
        out["cos" + name] = np.cos(ang).astype(np.float32).transpose(1, 0, 2).reshape(128, NPT * half)
        out["sin" + name] = np.sin(ang).astype(np.float32).transpose(1, 0, 2).reshape(128, NPT * half)
    return out


def kernel(x_prompt, x_sample, cache_k, cache_v, cache_idx_k, state_conv, c_prompt, c_sample,
           w_ada, b_ada, norm_g, w_in, q_norm_g, k_norm_g, conv_w, w_pa, w_pb, w_out):
    f = lambda a: np.ascontiguousarray(np.asarray(a, dtype=np.float32))
    x_prompt, x_sample, cache_k, cache_v, cache_idx_k, state_conv = map(f, (x_prompt, x_sample, cache_k, cache_v, cache_idx_k, state_conv))
    c_prompt, c_sample, w_ada, b_ada, norm_g, w_in = map(f, (c_prompt, c_sample, w_ada, b_ada, norm_g, w_in))
    q_norm_g, k_norm_g, conv_w, w_pa, w_pb, w_out = map(f, (q_norm_g, k_norm_g, conv_w, w_pa, w_pb, w_out))

    if "nc" not in _PROG:
        _PROG["nc"] = build_program()
    nc = _PROG["nc"]

    rt_ = _rope_tables()
    pf = np.zeros((128, NPF), np.float32)
    pf[:, O_COS16:O_COS16 + NPT * 16] = rt_["cos16"]
    pf[:, O_SIN16:O_SIN16 + NPT * 16] = rt_["sin16"]
    pf[:, O_COS8:O_COS8 + NPT * 8] = rt_["cos8"]
    pf[:, O_SIN8:O_SIN8 + NPT * 8] = rt_["sin8"]
    pf[:, O_POW2:O_POW2 + NIT + 2] = (2.0 ** (1.0 - np.arange(NIT + 2, dtype=np.float64))).astype(np.float32)[None, :]
    pf[:, O_GQ:O_GQ + 128] = q_norm_g[None, :]
    pf[:, O_GK:O_GK + 128] = k_norm_g[None, :]
    pf[:, O_BADA:O_BADA + 24] = b_ada.reshape(24, 128).T
    pf[:, O_NG:O_NG + 8] = norm_g.reshape(8, 128).T
    pf[:, O_CW:O_CW + 24] = conv_w.reshape(3, 8, 128).transpose(2, 0, 1).reshape(128, 24)
    pf[:, O_NH:O_NH + 16] = -0.5
    pb = np.zeros((128, 768), np.float32)
    eye = np.eye(128, dtype=np.float32)
    pb[:, 0:128] = eye
    pb[:, 128:640] = np.tile(eye, (1, 4))
    pb[:, 640:768] = 1.0
    pb = pb.astype(ml_dtypes.bfloat16)
    sel = np.zeros((3, 3, 128), np.float32)
    for s in range(3):
        sel[s, s, :] = 1.0
    sel = np.ascontiguousarray(sel.transpose(1, 0, 2).reshape(3, 384))
    bgate = np.ascontiguousarray(np.broadcast_to(b_ada[2048:3072][None, :], (3, D)))

    in_maps = []
    for c in range(8):
        cc = np.stack([c_prompt[2 * c], c_prompt[2 * c + 1], c_sample[c]], axis=0)
        cTm = np.ascontiguousarray(cc.reshape(3, 8, 128).transpose(2, 1, 0))
        sc = np.ascontiguousarray(state_conv[c].reshape(2, 8, 128).transpose(2, 1, 0))
        in_maps.append({
            "xp": np.ascontiguousarray(x_prompt[2 * c:2 * c + 2]),
            "xs": np.ascontiguousarray(x_sample[c]),
            "ck": np.ascontiguousarray(cache_k[c].reshape(PAST, 256)),
            "cvv": np.ascontiguousarray(cache_v[c].reshape(PAST, 256)),
            "cik": np.ascontiguousarray(cache_idx_k[c]),
            "sconv": sc, "cT": cTm,
            "w_ada": w_ada, "w_in": w_in, "w_pa": w_pa, "w_pb": w_pb, "w_out": w_out,
            "pf32": pf, "pbf": pb, "sel": sel, "bgate": bgate,
        })
    res = run_bass_kernel_spmd(nc, in_maps, core_ids=list(range(8)))
    R = res.results
    y_p = np.concatenate([R[c]["y_p"] for c in range(8)], axis=0)
    y_s = np.stack([R[c]["y_s"] for c in range(8)], axis=0)
    k_p = np.concatenate([R[c]["k_p"] for c in range(8)], axis=0).reshape(16, SEQ, 2, 128)
    v_p = np.concatenate([R[c]["v_p"] for c in range(8)], axis=0).reshape(16, SEQ, 2, 128)
    ki_p = np.concatenate([R[c]["ki_p"] for c in range(8)], axis=0)
    conv_p = np.concatenate([R[c]["conv_p"] for c in range(8)], axis=0)
    conv_p = np.ascontiguousarray(conv_p.transpose(0, 3, 2, 1).reshape(16, 2, D))
    k_s = np.stack([R[c]["k_s"] for c in range(8)], axis=0).reshape(8, DEC_SEQ, 2, 128)
    v_s = np.stack([R[c]["v_s"] for c in range(8)], axis=0).reshape(8, DEC_SEQ, 2, 128)
    ki_s = np.stack([R[c]["ki_s"] for c in range(8)], axis=0)
    conv_s = np.stack([R[c]["conv_s"] for c in range(8)], axis=0)
    conv_s = np.ascontiguousarray(conv_s.transpose(0, 3, 2, 1).reshape(8, 2, D))
    f32 = lambda a: np.ascontiguousarray(a, dtype=np.float32)
    return tuple(map(f32, (y_p, y_s, k_p, v_p, ki_p, conv_p, k_s, v_s, ki_s, conv_s)))
```

```python
import numpy as np
import ml_dtypes
import concourse.bass as bass
import concourse.mybir as mybir
from concourse.bass_utils import run_bass_kernel_spmd

F32 = mybir.dt.float32
BF16 = mybir.dt.bfloat16
ALU = mybir.AluOpType
AF = mybir.ActivationFunctionType
AX = mybir.AxisListType

D = 1024
SEQ = 2048
DEC_SEQ = 32
PAST = 1024
PROJ_W = 9288
NIT = 20
TOPK = 256.0
NEG = -30000.0
NEGBIG = -1.0e30
SCALE = 128.0 ** -0.5
WSC = (64.0 ** -0.5) * (8.0 ** -0.5)
EPS = 1e-6
SEQUENTIAL_RR = False
STRICT_SYNC = False
TRUNC = ''


class StopBuild(Exception):
    pass

NPT = 17

O_COS16 = 0
O_SIN16 = O_COS16 + NPT * 16
O_COS8 = O_SIN16 + NPT * 16
O_SIN8 = O_COS8 + NPT * 8
O_POW2 = O_SIN8 + NPT * 8
O_GQ = O_POW2 + (NIT + 2)
O_GK = O_GQ + 128
O_BADA = O_GK + 128
O_NG = O_BADA + 24
O_CW = O_NG + 8
O_NH = O_CW + 24
NPF = O_NH + 16


class Buf:
    __slots__ = ("name", "lw", "rd", "dsem", "dval", "dkey", "psum")

    def __init__(self, name):
        self.name = name
        self.psum = False
        self.lw = None
        self.rd = {}
        self.dsem = None
        self.dval = 0
        self.dkey = None


class Eng:
    def __init__(self, k, name, obj):
        self.name = name
        self.obj = obj
        self.sem = k.nc.alloc_semaphore("e_" + name)
        self.cnt = 0
        self.pending = False
        self.seen = {}


class K:
    def __init__(self, nc):
        self.nc = nc
        self.E = {}
        for n, o in (("pe", nc.tensor), ("act", nc.scalar), ("dve", nc.vector),
                     ("pool", nc.gpsimd), ("sp", nc.sync)):
            self.E[n] = Eng(self, n, o)
        self.dbufs = []

    def buf(self, name):
        return Buf(name)

    def bufs(self, name, n):
        return [Buf("%s%d" % (name, i)) for i in range(n)]

    def _wait(self, e, h):
        if h is None:
            return
        if h[0] == "e":
            key = h[1]
            sem = self.E[h[1]].sem
        else:
            key = h[1]
            sem = h[3]
        val = h[2]
        if e.seen.get(key, 0) >= val:
            return
        e.obj.wait_ge(sem, val)
        e.seen[key] = val

    def _deps(self, en, R, W, dkey=None):
        e = self.E[en]
        for b in R:
            if b.psum:
                for key_, h in b.rd.items():
                    if key_ != en:
                        self._wait(e, h)
            h = b.lw
            if h is None:
                continue
            if h[0] == "e" and h[1] == en and en == "pe":
                continue
            self._wait(e, h)
        for b in W:
            for j, h in enumerate([b.lw] + list(b.rd.values())):
                if h is None:
                    continue
                if h[0] == "e" and h[1] == en and (en in ("pe", "sp") or (not STRICT_SYNC and en != "pool")):
                    continue
                if j == 0 and h[0] == "d" and dkey is not None and h[1] == dkey:
                    continue
                self._wait(e, h)

    def I(self, en, fn, R=(), W=(), sig=True):
        e = self.E[en]
        self._deps(en, R, W)
        inst = fn(e.obj)
        if sig:
            e.cnt += 1
            inst.then_inc(e.sem, 1)
            e.pending = False
            h = ("e", en, e.cnt)
        else:
            e.pending = True
            h = ("e", en, e.cnt + 1)
        for b in W:
            b.lw = h
            b.rd = {}
        for b in R:
            b.rd[en] = h
        return inst

    def dma(self, qn, out_ap, in_ap, R=(), W=(), sb=None, **kw):
        e = self.E[qn]
        if sb.dsem is None:
            sb.dkey = "d%d" % len(self.dbufs)
            sb.dsem = self.nc.alloc_semaphore(sb.dkey)
            self.dbufs.append(sb)
        self._deps(qn, R, W, dkey=sb.dkey)
        inst = e.obj.dma_start(out=out_ap, in_=in_ap, **kw)
        sb.dval += 16
        inst.then_inc(sb.dsem, 16)
        h = ("d", sb.dkey, sb.dval, sb.dsem)
        for b in W:
            b.lw = h
            b.rd = {}
        for b in R:
            b.rd[sb.dkey] = h
        return inst

    def finish(self, en="sp"):
        e = self.E[en]
        assert not self.E["pe"].pending
        for sb in self.dbufs:
            self._wait(e, ("d", sb.dkey, sb.dval, sb.dsem))


def build_program():
    nc = bass.Bass("TRN2", target_bir_lowering=False)
    k = K(nc)

    def din(name, shape, dt=F32):
        return nc.dram_tensor(name, list(shape), dt, kind="ExternalInput").ap()

    def dout(name, shape, dt=F32):
        return nc.dram_tensor(name, list(shape), dt, kind="ExternalOutput").ap()

    def sb(name, shape, dt=F32):
        return nc.alloc_sbuf_tensor("s_" + name, list(shape), dt)

    xp = din("xp", [2, SEQ, D])
    xs = din("xs", [DEC_SEQ, D])
    ck = din("ck", [PAST, 256])
    cvv = din("cvv", [PAST, 256])
    cik = din("cik", [PAST, 64])
    sconv = din("sconv", [128, 8, 2])
    cT_d = din("cT", [128, 8, 3])
    w_ada = din("w_ada", [D, 3 * D])
    w_in = din("w_in", [D, PROJ_W])
    w_pa = din("w_pa", [D, D])
    w_pb = din("w_pb", [D, D])
    w_out = din("w_out", [D, D])
    pf_d = din("pf32", [128, NPF])
    pb_d = din("pbf", [128, 768], BF16)
    sel_d = din("sel", [3, 3 * 128])
    bgate_d = din("bgate", [3, D])

    y_p = dout("y_p", [2, SEQ, D])
    y_s = dout("y_s", [DEC_SEQ, D])
    k_p = dout("k_p", [2, SEQ, 256])
    v_p = dout("v_p", [2, SEQ, 256])
    ki_p = dout("ki_p", [2, SEQ, 64])
    conv_p = dout("conv_p", [2, 128, 8, 2])
    k_s = dout("k_s", [DEC_SEQ, 256])
    v_s = dout("v_s", [DEC_SEQ, 256])
    ki_s = dout("ki_s", [DEC_SEQ, 64])
    conv_s = dout("conv_s", [128, 8, 2])

    OFF = {"q": 0, "k": 1024, "v": 1280, "za": 1536, "qi": 2560, "ki": 3072, "wi": 3136,
           "u": 3144, "bg": 4168, "cg": 5192, "zb": 6216, "ga": 7240, "gb": 8264}
    blocks = {}
    Bw_src = k.buf("wsrc")
    grp = {g: k.buf("wg_" + g) for g in ("ada", "tm", "fm", "tail")}

    def add_block(name, src, g):
        ncols = src.shape[1]
        scr = nc.dram_tensor("scr_" + name, [D, ncols], BF16, kind="Internal").ap()
        gb = k.buf("wb_" + name) if g in ("ada", "tm") else grp[g]
        k.dma("pool", scr, src, R=[Bw_src], W=[gb], sb=gb)
        blocks[name] = (scr, gb, ncols)

    for j in range(6):
        add_block("ada%d" % j, w_ada[:, j * 512:(j + 1) * 512], "ada")
    add_block("q0", w_in[:, 0:512], "tm")
    add_block("q1", w_in[:, 512:1024], "tm")
    add_block("kv", w_in[:, 1024:1536], "tm")
    add_block("qi", w_in[:, 2560:3072], "tm")
    add_block("kiwi", w_in[:, 3072:3144], "tm")
    for nm in ("za",):
        for hf in range(2):
            add_block("%s%d" % (nm, hf), w_in[:, OFF[nm] + hf * 512: OFF[nm] + (hf + 1) * 512], "fm")
    for hf in range(2):
        for nm in ("u", "cg", "bg", "zb"):
            add_block("%s%d" % (nm, hf), w_in[:, OFF[nm] + hf * 512: OFF[nm] + (hf + 1) * 512], "fm")
    for hf in range(2):
        add_block("pa%d" % hf, w_pa[:, hf * 512:(hf + 1) * 512], "tail")
    for nm in ("ga", "gb"):
        for hf in range(2):
            add_block("%s%d" % (nm, hf), w_in[:, OFF[nm] + hf * 512: OFF[nm] + (hf + 1) * 512], "tail")
    for hf in range(2):
        add_block("pb%d" % hf, w_pb[:, hf * 512:(hf + 1) * 512], "tail")
    for hf in range(2):
        add_block("out%d" % hf, w_out[:, hf * 512:(hf + 1) * 512], "tail")

    PASS_BLOCKS = (["q0", "q1", "kv", "qi", "kiwi", "za0", "za1"]
                   + ["%s%d" % (nm, hf) for hf in range(2) for nm in ("u", "cg", "bg", "zb")]
                   + ["gb0", "pb0", "gb1", "pb1", "ga0", "pa0", "ga1", "pa1", "out0", "out1"])
    NPASS = 9
    stream = ["ada%d" % j for j in range(6)] + PASS_BLOCKS * NPASS

    NRING = 3
    ring = [sb("ring%d" % i, [128, 8, 512], BF16) for i in range(NRING)]
    Bring = k.bufs("ring", NRING)
    st = {"issued": 0, "pos": 0}

    def issue_to(n):
        while st["issued"] < min(n, len(stream)):
            i = st["issued"]
            scr, g, ncols = blocks[stream[i]]
            s = i % NRING
            k.dma("sp", ring[s][:, :, 0:ncols], scr.rearrange("(kc p) c -> p kc c", p=128),
                  R=[g], W=[Bring[s]], sb=Bring[s])
            st["issued"] += 1

    def next_block(expect):
        i = st["pos"]
        assert stream[i] == expect, (stream[i], expect)
        issue_to(i + NRING)
        st["pos"] += 1
        return ring[i % NRING], Bring[i % NRING]

    pf = sb("pf", [128, NPF])
    pbc = sb("pbc", [128, 768], BF16)
    sel = sb("sel", [3, 384])

    cT = sb("cT", [128, 8, 3])
    scT = sb("scT", [128, 8, 3], BF16)
    cth = sb("cth", [128, 8, 3])
    modfm = sb("modfm", [128, 16, 3])
    Amod = sb("Amod", [128, 8, 3])
    gate_tm = sb("gate_tm", [3, D])
    gate_bc = sb("gate_bc", [128, D])
    xt = [sb("xt%d" % i, [128, D]) for i in range(2)]

    hT = sb("hT", [128, 8, 512], BF16)
    QT = sb("QT", [128, 8, 512], BF16)
    KT = sb("KT", [128, 2, SEQ], BF16)
    Vb = sb("Vb", [128, 16, 256], BF16)
    kiT = sb("kiT", [128, SEQ], BF16)
    qiT = sb("qiT", [128, 4, 512], BF16)
    wi_s = sb("wi_s", [128, 4, 8])
    zaT = sb("zaT", [128, 8, 512], BF16)
    cvb = sb("cvb", [128, 4, 514])
    cvh = sb("cvh", [128, 8, 2])
    ycb = sb("ycb", [128, 4, 512])
    BvT = sb("BvT", [128, 8, 512], BF16)
    mT = sb("mT", [128, 8, 512], BF16)
    scoresS = [sb("scores%d" % i, [128, SEQ]) for i in range(2)]
    scores = scoresS[0]
    junk = sb("junk", [128, SEQ], mybir.dt.uint8)
    junk2 = sb("junk2", [128, SEQ], mybir.dt.uint8)
    bgate = scoresS[1][0:3, 0:D]
    negmS = [sb("negm%d" % i, [128, SEQ], BF16) for i in range(2)]
    rl = [sb("rl%d" % i, [128, 512]) for i in range(2)]
    bisS = [sb("bis%d" % i, [128, 8]) for i in range(2)]
    stpS = [sb("stp%d" % i, [128, NIT + 2]) for i in range(2)]
    PT = [sb("PT%d" % i, [128, 4, 128], BF16) for i in range(3)]
    qf = [sb("qf%d" % i, [128, 512]) for i in range(4)]
    qb = [sb("qb%d" % i, [128, 512], BF16) for i in range(4)]
    rt = [sb("rt%d" % i, [128, 256]) for i in range(4)]
    stat = sb("stat", [128, 8, 12])
    kvo = [sb("kvo%d" % i, [128, 512]) for i in range(4)]
    kio = [sb("kio%d" % i, [128, 64]) for i in range(4)]
    yst = rl
    tht = [sb("tht%d" % i, [128, 512]) for i in range(2)]
    tmpf = [sb("tmpf%d" % i, [128, 512]) for i in range(2)]
    xn = [tht[i][:].bitcast(BF16) for i in range(2)]
    ss = sb("ss", [128, 8])
    ms = sb("ms", [128, 8])
    rstd = sb("rstd", [128, 8])

    cstg = scores[:].rearrange("p (b c) -> p b c", c=256)
    GAh = sb("GAh", [128, 4, 512], BF16)
    GBh = sb("GBh", [128, 4, 512], BF16)
    yc2 = sb("yc2", [128, 512])

    ident = pbc[:, 0:128]
    I4 = pbc[:, 128:640]
    ones = pbc[:, 640:768]

    MM = [nc.alloc_psum_tensor("MM%d" % i, [128, 512], F32) for i in range(2)]
    TP = [nc.alloc_psum_tensor("TP%d" % i, [128, 1024], BF16) for i in range(2)]
    SP_ = [nc.alloc_psum_tensor("SS%d" % i, [128, 4, 128], F32) for i in range(2)]
    OTp = nc.alloc_psum_tensor("OTp", [128, 4, 128], F32)
    SMp = nc.alloc_psum_tensor("SMp", [128, 4, 128], F32)
    TPf = [TP[i][:].bitcast(F32) for i in range(2)]

    BMM = k.bufs("MM", 2)
    BTP = k.bufs("TP", 2)
    BSP = k.bufs("SP", 2)
    BOT = k.buf("OT")
    BSM = k.buf("SM")
    for b_ in BMM + BTP + BSP + [BOT, BSM]:
        b_.psum = True
    PB6 = [(SP_[0][:].rearrange("p h t -> p (h t)"), BSP[0]), (SP_[1][:].rearrange("p h t -> p (h t)"), BSP[1]),
           (OTp[:].rearrange("p h t -> p (h t)"), BOT), (SMp[:].rearrange("p h t -> p (h t)"), BSM)]
    TP4 = [(TP[0][:], BTP[0]), (TP[1][:], BTP[1]), (MM[0][:].bitcast(BF16), BMM[0]), (MM[1][:].bitcast(BF16), BMM[1])]
    rr = {"tp4": 0, "xt": 0, "qf": 0, "st": 0, "pb6": 0, "mm": 0, "tp": 0, "sp": 0, "pt": 0, "rl": 0, "q": 0, "th": 0, "tf": 0, "yst": 0, "kvo": 0, "kio": 0}

    def rot(key, n):
        v = rr[key]
        rr[key] = (v + 1) % n
        return v

    Bc = k.buf("consts")
    Bmod = k.buf("mod")
    Bgbc = k.buf("gate_bc")
    Bxt = k.bufs("xt", 2)

    BhT = k.bufs("hT", 4)
    BQT = k.bufs("QT", 4)
    BKT = k.buf("KT")
    BV = k.buf("V")
    BkiT = k.buf("kiT")
    BqiT = k.bufs("qiT", 4)
    Bwi = k.buf("wi")
    BzaT = k.bufs("zaT", 8)
    Bcv = k.buf("cv")
    Bcvh = k.buf("cvh")
    Byc = k.buf("yc")
    BBv = k.bufs("Bv", 8)
    BmT = k.bufs("mT", 8)
    BscS = [k.bufs("sc%d_" % i, 4) for i in range(2)]
    Bsc = BscS[0]
    BnegmS = k.bufs("negm", 2)
    BbisS = k.bufs("bis", 2)
    Bjunk = k.buf("junk")
    Bjunk2 = k.buf("junk2")

    Brl = k.bufs("rl", 2)
    BPT = k.bufs("PT", 3)
    Bqf = k.bufs("qf", 4)
    Bqb = k.bufs("qb", 4)
    Brt = k.bufs("rt", 4)
    Bst = k.bufs("st", 8)
    Bkvo = k.bufs("kvo", 4)
    Bkio = k.bufs("kio", 4)
    Byst = Brl
    Btht = k.bufs("tht", 2)
    Bxn = Btht
    Btmpf = k.bufs("tmpf", 2)
    Bss = k.buf("ss")

    Bcst = Bsc
    BGA = k.buf("GA")
    BGB = k.buf("GB")
    Byc2 = k.buf("yc2")
    Bout = k.buf("dram_out")
    Bin = k.buf("dram_in")

    k.dma("sp", pf[:], pf_d, R=[Bin], W=[Bc], sb=Bc)
    k.dma("sp", pbc[:], pb_d, R=[Bin], W=[Bc], sb=Bc)
    k.dma("sp", sel[:], sel_d, R=[Bin], W=[Bc], sb=Bc)
    k.dma("sp", bgate, bgate_d, R=[Bin], W=BscS[1], sb=BscS[1][0])
    k.dma("sp", cT[:], cT_d, R=[Bin], W=[Bc], sb=Bc)
    issue_to(NRING)

    k.I("act", lambda e: e.activation(out=cth[:], in_=cT[:], func=AF.Tanh, scale=0.5), R=[Bc], W=[Bmod])
    k.I("dve", lambda e: e.scalar_tensor_tensor(out=cth[:], in0=cth[:], scalar=1.0, in1=cT[:], op0=ALU.add, op1=ALU.mult),
        R=[Bmod, Bc], W=[Bmod])
    k.I("dve", lambda e: e.tensor_scalar(scT[:], cth[:], 0.5, None, op0=ALU.mult), R=[Bmod], W=[Bmod])
    for j in range(6):
        wt, bw = next_block("ada%d" % j)
        if j < 4:
            i = rot("mm", 2)
            for c in range(4):
                for kc in range(8):
                    k.I("pe", lambda e, c=c, kc=kc: e.matmul(MM[i][:, c * 3:(c + 1) * 3], lhsT=wt[:, kc, c * 128:(c + 1) * 128],
                                                            rhs=scT[:, kc, :], start=(kc == 0), stop=(kc == 7)),
                        R=[bw, Bmod], W=[BMM[i]], sig=(kc == 7))
            k.I("dve", lambda e, j=j: e.tensor_tensor(
                out=modfm[:, 4 * j:4 * j + 4, :], in0=MM[i][:, 0:12].rearrange("p (c s) -> p c s", s=3),
                in1=pf[:, O_BADA + 4 * j:O_BADA + 4 * j + 4].unsqueeze(2).to_broadcast([128, 4, 3]), op=ALU.add),
                R=[BMM[i], Bc], W=[Bmod])
        else:
            i = rot("mm", 2)
            for kc in range(8):
                k.I("pe", lambda e, kc=kc: e.matmul(MM[i][0:3, :], lhsT=scT[:, kc, :], rhs=wt[:, kc, :],
                                                    start=(kc == 0), stop=(kc == 7)),
                    R=[bw, Bmod], W=[BMM[i]], sig=(kc == 7))
            c0 = (j - 4) * 512
            k.I("dve", lambda e, c0=c0: e.tensor_tensor(out=gate_tm[:, c0:c0 + 512], in0=MM[i][0:3, :],
                                                        in1=bgate[:, c0:c0 + 512], op=ALU.add),
                R=[BMM[i], Bc] + BscS[1], W=[Bmod])
    k.I("dve", lambda e: e.scalar_tensor_tensor(
        out=Amod[:], in0=modfm[:, 8:16, :], scalar=1.0,
        in1=pf[:, O_NG:O_NG + 8].unsqueeze(2).to_broadcast([128, 8, 3]), op0=ALU.add, op1=ALU.mult),
        R=[Bmod, Bc], W=[Bmod])
    k.I("dve", lambda e: e.tensor_scalar(gate_tm[:], gate_tm[:], 0.5, None, op0=ALU.mult), R=[Bmod], W=[Bmod])

    def rrobin(gens):
        gens = list(gens)
        if SEQUENTIAL_RR:
            for g_ in gens:
                for _ in g_:
                    pass
            return
        while gens:
            for g_ in list(gens):
                try:
                    next(g_)
                except StopIteration:
                    gens.remove(g_)

    def rope(eng, buf3, nh, half, cos, sin, Bb, rs):
        x1 = buf3[:, 0:nh, 0:half]
        x2 = buf3[:, 0:nh, half:2 * half]
        cb = cos.unsqueeze(1).to_broadcast([128, nh, half])
        sn = sin.unsqueeze(1).to_broadcast([128, nh, half])
        t = [rt[rs][:, i * 64:i * 64 + nh * half].rearrange("p (h d) -> p h d", d=half) for i in range(4)]
        k.I(eng, lambda e: e.tensor_tensor(out=t[0], in0=x1, in1=cb, op=ALU.mult), R=[Bb, Bc], W=[Brt[rs]])
        k.I(eng, lambda e: e.tensor_tensor(out=t[1], in0=x2, in1=sn, op=ALU.mult), R=[Bb, Bc], W=[Brt[rs]])
        k.I(eng, lambda e: e.tensor_tensor(out=t[2], in0=x2, in1=cb, op=ALU.mult), R=[Bb, Bc], W=[Brt[rs]])
        k.I(eng, lambda e: e.tensor_tensor(out=t[3], in0=x1, in1=sn, op=ALU.mult), R=[Bb, Bc], W=[Brt[rs]])
        k.I(eng, lambda e: e.tensor_tensor(out=x1, in0=t[0], in1=t[1], op=ALU.subtract), R=[Brt[rs]], W=[Bb])
        k.I(eng, lambda e: e.tensor_tensor(out=x2, in0=t[2], in1=t[3], op=ALU.add), R=[Brt[rs]], W=[Bb])

    def seq_pass(seq, p, NT, x_src, past, ptile0, outs, last):
        N = NT * 128
        nrow = 128 if seq < 2 else DEC_SEQ

        def gen_p1(t):
            xi = rot("xt", 2)
            sj = rot("st", 8)
            if nrow < 128:
                k.I("dve", lambda e: e.memset(xt[xi][:], 0.0), W=[Bxt[xi]])
            k.dma("sp", xt[xi][0:nrow, :], x_src(t), R=[Bin], W=[Bxt[xi]], sb=Bxt[xi])
            k.I("act", lambda e: e.activation(out=xn[xi][:], in_=xt[xi][:], func=AF.Square, accum_out=stat[:, sj, 0:1]),
                R=[Bxt[xi]], W=[Bxn[xi], Bst[sj]])
            yield
            k.I("dve", lambda e: e.tensor_scalar(stat[:, sj, 4:5], stat[:, sj, 0:1], 1.0 / D, EPS, op0=ALU.mult, op1=ALU.add),
                R=[Bst[sj]], W=[Bst[sj]])
            yield
            k.I("pool", lambda e: e.tensor_tensor(out=stat[:, sj, 8:9], in0=stat[:, sj, 4:5], in1=pf[:, O_NH:O_NH + 1], op=ALU.pow),
                R=[Bst[sj], Bc], W=[Bst[sj]])
            yield
            k.I("dve", lambda e: e.tensor_scalar(xn[xi][:], xt[xi][:], stat[:, sj, 8:9], None, op0=ALU.mult),
                R=[Bxt[xi], Bst[sj]], W=[Bxn[xi]])
            yield
            ti = rot("tp", 2)
            for kc in range(8):
                k.I("pe", lambda e, kc=kc: e.transpose(TP[ti][:, kc * 128:(kc + 1) * 128], xn[xi][:, kc * 128:(kc + 1) * 128], ident),
                    R=[Bxn[xi], Bc], W=[BTP[ti]], sig=(kc == 7))
            yield
            for kc in range(8):
                o = hT[:, kc, t * 128:(t + 1) * 128]
                i_ = TP[ti][:, kc * 128:(kc + 1) * 128]
                if t % 2 == 0:
                    k.I("act", lambda e: e.activation(out=o, in_=i_, func=AF.Identity,
                                                      scale=Amod[:, kc, seq:seq + 1], bias=modfm[:, kc, seq:seq + 1]),
                        R=[BTP[ti], Bmod], W=[BhT[t]])
                else:
                    k.I("dve", lambda e: e.tensor_scalar(o, i_, Amod[:, kc, seq:seq + 1], modfm[:, kc, seq:seq + 1],
                                                         op0=ALU.mult, op1=ALU.add),
                        R=[BTP[ti], Bmod], W=[BhT[t]])

        for t0 in range(0, NT, 2):
            rrobin([gen_p1(t) for t in range(t0, min(NT, t0 + 2))])

        if TRUNC == "p1":
            raise StopBuild()
        def tm_mm(wt, bw, t, ncols):
            i = rot("pb6", 4)
            o, Bo = PB6[i]
            for kc in range(8):
                k.I("pe", lambda e, kc=kc: e.matmul(o[:, 0:ncols], lhsT=hT[:, kc, t * 128:(t + 1) * 128], rhs=wt[:, kc, 0:ncols],
                                                    start=(kc == 0), stop=(kc == 7)),
                    R=[bw, BhT[t]], W=[Bo], sig=(kc == 7))
            return o, Bo

        def norm_heads(o, Bo, nh, gofs, qi_, sj):
            for h in range(nh):
                k.I("act", lambda e, h=h: e.activation(out=junk[:, h * 128:(h + 1) * 128], in_=o[:, h * 128:(h + 1) * 128],
                                                       func=AF.Square, accum_out=stat[:, sj, h:h + 1]),
                    R=[Bo], W=[Bjunk, Bst[sj]])
            yield
            k.I("dve", lambda e: e.tensor_scalar(stat[:, sj, 4:4 + nh], stat[:, sj, 0:nh], 1.0 / 128, EPS, op0=ALU.mult, op1=ALU.add),
                R=[Bst[sj]], W=[Bst[sj]])
            yield
            k.I("pool", lambda e: e.tensor_tensor(out=stat[:, sj, 8:8 + nh], in0=stat[:, sj, 4:4 + nh], in1=pf[:, O_NH:O_NH + nh], op=ALU.pow),
                R=[Bst[sj], Bc], W=[Bst[sj]])
            yield
            for h in range(nh):
                k.I("dve", lambda e, h=h: e.scalar_tensor_tensor(out=qf[qi_][:, h * 128:(h + 1) * 128], in0=o[:, h * 128:(h + 1) * 128],
                                                                 scalar=stat[:, sj, 8 + h:9 + h], in1=pf[:, gofs:gofs + 128],
                                                                 op0=ALU.mult, op1=ALU.mult),
                    R=[Bo, Bst[sj], Bc], W=[Bqf[qi_]])

        def cs16(t):
            o = (ptile0 + t) * 16
            return pf[:, O_COS16 + o:O_COS16 + o + 16], pf[:, O_SIN16 + o:O_SIN16 + o + 16]

        def cs8(t):
            o = (ptile0 + t) * 8
            return pf[:, O_COS8 + o:O_COS8 + o + 8], pf[:, O_SIN8 + o:O_SIN8 + o + 8]

        def kbof(t):
            return ((past // 128) + p * 4 + t) if seq < 2 else past // 128

        def gen_q(g, wt, bw, t):
            o, Bo = tm_mm(wt, bw, t, 512)
            qi_ = rot("qf", 4)
            sj = rot("st", 8)
            yield
            yield from norm_heads(o, Bo, 4, O_GQ, qi_, sj)
            c16, s16 = cs16(t)
            rope("dve", qf[qi_][:].rearrange("p (h d) -> p h d", d=128), 4, 16, c16, s16, Bqf[qi_], qi_)
            yield
            k.I("act", lambda e: e.activation(out=qb[qi_][:], in_=qf[qi_][:], func=AF.Copy), R=[Bqf[qi_]], W=[Bqb[qi_]])
            yield
            tpo, Btp = TP4[rot("tp4", 4)]
            for h in range(4):
                k.I("pe", lambda e, h=h: e.transpose(tpo[:, h * 128:(h + 1) * 128], qb[qi_][:, h * 128:(h + 1) * 128], ident),
                    R=[Bqb[qi_], Bc], W=[Btp], sig=(h == 3))
            yield
            k.I("act", lambda e: e.activation(out=QT[:, 4 * g:4 * g + 4, t * 128:(t + 1) * 128],
                                              in_=tpo[:, 0:512].rearrange("p (h t) -> p h t", t=128), func=AF.Copy),
                R=[Btp], W=[BQT[t]])

        def gen_kv(wt, bw, t):
            o, Bo = tm_mm(wt, bw, t, 512)
            qi_ = rot("qf", 4)
            sj = rot("st", 8)
            ko = rot("kvo", 4)
            kb = kbof(t)
            yield
            k.I("act", lambda e: e.activation(out=kvo[ko][:, 256:512], in_=o[:, 256:512], func=AF.Copy), R=[Bo], W=[Bkvo[ko]])
            k.I("dve", lambda e: e.tensor_copy(Vb[:, kb, :], kvo[ko][:, 256:512]), R=[Bkvo[ko]], W=[BV])
            yield from norm_heads(o, Bo, 2, O_GK, qi_, sj)
            c16, s16 = cs16(t)
            rope("dve", qf[qi_][:, 0:256].rearrange("p (h d) -> p h d", d=128), 2, 16, c16, s16, Bqf[qi_], qi_)
            yield
            k.I("act", lambda e: e.activation(out=kvo[ko][:, 0:256], in_=qf[qi_][:, 0:256], func=AF.Copy), R=[Bqf[qi_]], W=[Bkvo[ko]])
            k.I("dve", lambda e: e.tensor_copy(qb[qi_][:, 0:256], qf[qi_][:, 0:256]), R=[Bqf[qi_]], W=[Bqb[qi_]])
            yield
            tpo, Btp = TP4[rot("tp4", 4)]
            for h in range(2):
                k.I("pe", lambda e, h=h: e.transpose(tpo[:, h * 128:(h + 1) * 128], qb[qi_][:, h * 128:(h + 1) * 128], ident),
                    R=[Bqb[qi_], Bc], W=[Btp], sig=(h == 1))
            ok, ov, oki = outs["kv"](t)
            k.dma("act", ok, kvo[ko][0:nrow, 0:256], R=[Bkvo[ko]], sb=Bkvo[ko])
            k.dma("act", ov, kvo[ko][0:nrow, 256:512], R=[Bkvo[ko]], sb=Bkvo[ko])
            yield
            k.I("act", lambda e: e.activation(out=KT[:, 0:2, kb * 128:(kb + 1) * 128],
                                              in_=tpo[:, 0:256].rearrange("p (h t) -> p h t", t=128), func=AF.Copy),
                R=[Btp], W=[BKT])

        def gen_qi(wt, bw, t):
            o, Bo = tm_mm(wt, bw, t, 512)
            qi_ = rot("qf", 4)
            yield
            k.I("act", lambda e: e.activation(out=qf[qi_][:], in_=o[:], func=AF.Copy), R=[Bo], W=[Bqf[qi_]])
            yield
            c8, s8 = cs8(t)
            rope("dve", qf[qi_][:].rearrange("p (h d) -> p h d", d=64), 8, 8, c8, s8, Bqf[qi_], qi_)
            yield
            k.I("act", lambda e: e.activation(out=qb[qi_][:], in_=qf[qi_][:], func=AF.Copy), R=[Bqf[qi_]], W=[Bqb[qi_]])
            yield
            tpo, Btp = TP4[rot("tp4", 4)]
            for h in range(4):
                k.I("pe", lambda e, h=h: e.transpose(tpo[:, h * 128:(h + 1) * 128], qb[qi_][:, h * 128:(h + 1) * 128], ident),
                    R=[Bqb[qi_], Bc], W=[Btp], sig=(h == 3))
            yield
            k.I("act", lambda e: e.activation(out=qiT[:, 0:4, t * 128:(t + 1) * 128],
                                              in_=tpo[:, 0:512].rearrange("p (h t) -> p h t", t=128), func=AF.Copy),
                R=[Btp], W=[BqiT[t]])

        def gen_kiwi(wt, bw, t):
            o, Bo = tm_mm(wt, bw, t, 72)
            qi_ = rot("qf", 4)
            ko = rot("kio", 4)
            kb = kbof(t)
            yield
            k.I("act", lambda e: e.activation(out=qf[qi_][:, 0:64], in_=o[:, 0:64], func=AF.Copy), R=[Bo], W=[Bqf[qi_]])
            k.I("act", lambda e: e.activation(out=wi_s[:, t, :], in_=o[:, 64:72], func=AF.Identity, scale=WSC), R=[Bo], W=[Bwi])
            yield
            c8, s8 = cs8(t)
            rope("dve", qf[qi_][:, 0:64].rearrange("p (h d) -> p h d", d=64), 1, 8, c8, s8, Bqf[qi_], qi_)
            yield
            k.I("act", lambda e: e.activation(out=kio[ko][:], in_=qf[qi_][:, 0:64], func=AF.Copy), R=[Bqf[qi_]], W=[Bkio[ko]])
            k.I("dve", lambda e: e.tensor_copy(qb[qi_][:, 0:64], qf[qi_][:, 0:64]), R=[Bqf[qi_]], W=[Bqb[qi_]])
            k.I("dve", lambda e: e.tensor_copy(qb[qi_][:, 64:128], qf[qi_][:, 0:64]), R=[Bqf[qi_]], W=[Bqb[qi_]])
            yield
            tpo, Btp = TP4[rot("tp4", 4)]
            k.I("pe", lambda e: e.transpose(tpo[:, 0:128], qb[qi_][:, 0:128], ident), R=[Bqb[qi_], Bc], W=[Btp])
            ok, ov, oki = outs["kv"](t)
            k.dma("act", oki, kio[ko][0:nrow, :], R=[Bkio[ko]], sb=Bkio[ko])
            yield
            k.I("act", lambda e: e.activation(out=kiT[:, kb * 128:(kb + 1) * 128], in_=tpo[:, 0:128], func=AF.Copy),
                R=[Btp], W=[BkiT])

        for g in range(2):
            wt, bw = next_block("q%d" % g)
            rrobin([gen_q(g, wt, bw, t) for t in range(NT)])
        if TRUNC == "q":
            raise StopBuild()
        wt, bw = next_block("kv")
        rrobin([gen_kv(wt, bw, t) for t in range(NT)])
        if TRUNC == "kv":
            raise StopBuild()
        wt, bw = next_block("qi")
        rrobin([gen_qi(wt, bw, t) for t in range(NT)])
        if TRUNC == "qi":
            raise StopBuild()
        wt, bw = next_block("kiwi")
        rrobin([gen_kiwi(wt, bw, t) for t in range(NT)])

        if TRUNC == "p2":
            raise StopBuild()
        def fm_mm(wt, bw, c):
            i = rot("tp", 2)
            o = TPf[i]
            for kc in range(8):
                k.I("pe", lambda e, kc=kc: e.matmul(o[:, 0:N], lhsT=wt[:, kc, c * 128:(c + 1) * 128], rhs=hT[:, kc, 0:N],
                                                    start=(kc == 0), stop=(kc == 7)),
                    R=[bw] + BhT[0:NT], W=[BTP[i]], sig=(kc == 7))
            return o, BTP[i]

        def tail_mm(wt, bw, c, src, Bsrc):
            i = rot("tp", 2)
            o = TPf[i]
            for kc in range(8):
                k.I("pe", lambda e, kc=kc: e.matmul(o[:, 0:N], lhsT=wt[:, kc, c * 128:(c + 1) * 128], rhs=src[:, kc, 0:N],
                                                    start=(kc == 0), stop=(kc == 7)),
                    R=[bw] + Bsrc, W=[BTP[i]], sig=(kc == 7))
            return o, BTP[i]

        def gen_za():
            for hf in range(2):
                wt, bw = next_block("za%d" % hf)
                for c in range(4):
                    o, Bo = fm_mm(wt, bw, c)
                    th = rot("th", 2)
                    ch = hf * 4 + c
                    k.I("act", lambda e: e.activation(out=tht[th][:, 0:N], in_=o[:, 0:N], func=AF.Tanh, scale=0.5),
                        R=[Bo], W=[Btht[th]])
                    k.I("dve", lambda e: e.scalar_tensor_tensor(out=zaT[:, ch, 0:N], in0=tht[th][:, 0:N], scalar=1.0, in1=o[:, 0:N],
                                                                op0=ALU.add, op1=ALU.mult),
                        R=[Btht[th], Bo], W=[BzaT[ch]])
                    yield "y"

        def gen_Y():
            for hf in range(2):
                wt, bw = next_block("u%d" % hf)
                k.I("pool", lambda e: e.tensor_copy(cvb[:, :, 0:2], cvh[:, hf * 4:hf * 4 + 4, :]), R=[Bcvh], W=[Bcv])
                for c in range(4):
                    o, Bo = fm_mm(wt, bw, c)
                    k.I("act", lambda e: e.activation(out=cvb[:, c, 2:2 + N], in_=o[:, 0:N], func=AF.Copy), R=[Bo], W=[Bcv])
                    yield "y"
                wt, bw = next_block("cg%d" % hf)
                for c in range(4):
                    o, Bo = fm_mm(wt, bw, c)
                    k.I("dve", lambda e: e.tensor_tensor(out=cvb[:, c, 2:2 + N], in0=o[:, 0:N], in1=cvb[:, c, 2:2 + N], op=ALU.mult),
                        R=[Bo, Bcv], W=[Bcv])
                    yield "y"
                nv = N if seq < 2 else DEC_SEQ
                k.I("pool", lambda e: e.tensor_copy(cvh[:, hf * 4:hf * 4 + 4, :], cvb[:, :, nv:nv + 2]), R=[Bcv], W=[Bcvh])
                for c in range(4):
                    ch = hf * 4 + c
                    w0 = pf[:, O_CW + 0 * 8 + ch:O_CW + 0 * 8 + ch + 1]
                    w1 = pf[:, O_CW + 1 * 8 + ch:O_CW + 1 * 8 + ch + 1]
                    w2 = pf[:, O_CW + 2 * 8 + ch:O_CW + 2 * 8 + ch + 1]
                    k.I("pool", lambda e: e.tensor_scalar(ycb[:, c, 0:N], cvb[:, c, 0:N], w0, 0.0, op0=ALU.mult, op1=ALU.add), R=[Bcv, Bc], W=[Byc])
                    k.I("pool", lambda e: e.tensor_scalar(yc2[:, 0:N], cvb[:, c, 1:1 + N], w1, 0.0, op0=ALU.mult, op1=ALU.add), R=[Bcv, Bc], W=[Byc2])
                    k.I("pool", lambda e: e.tensor_tensor(out=ycb[:, c, 0:N], in0=ycb[:, c, 0:N], in1=yc2[:, 0:N], op=ALU.add), R=[Byc, Byc2], W=[Byc])
                    k.I("pool", lambda e: e.tensor_scalar(yc2[:, 0:N], cvb[:, c, 2:2 + N], w2, 0.0, op0=ALU.mult, op1=ALU.add), R=[Bcv, Bc], W=[Byc2])
                    k.I("pool", lambda e: e.tensor_tensor(out=ycb[:, c, 0:N], in0=ycb[:, c, 0:N], in1=yc2[:, 0:N], op=ALU.add), R=[Byc, Byc2], W=[Byc])
                    yield "y"
                wt, bw = next_block("bg%d" % hf)
                for c in range(4):
                    o, Bo = fm_mm(wt, bw, c)
                    k.I("dve", lambda e: e.scalar_tensor_tensor(out=ycb[:, c, 0:N], in0=o[:, 0:N], scalar=0.5, in1=ycb[:, c, 0:N],
                                                                op0=ALU.mult, op1=ALU.mult),
                        R=[Bo, Byc], W=[Byc])
                    yield "y"
                wt, bw = next_block("zb%d" % hf)
                for c in range(4):
                    o, Bo = fm_mm(wt, bw, c)
                    th = rot("th", 2)
                    ch = hf * 4 + c
                    k.I("act", lambda e: e.activation(out=tht[th][:, 0:N], in_=o[:, 0:N], func=AF.Tanh, scale=0.5),
                        R=[Bo], W=[Btht[th]])
                    k.I("dve", lambda e: e.scalar_tensor_tensor(out=tht[th][:, 0:N], in0=tht[th][:, 0:N], scalar=1.0, in1=o[:, 0:N],
                                                                op0=ALU.add, op1=ALU.mult),
                        R=[Btht[th], Bo], W=[Btht[th]])
                    k.I("pool", lambda e: e.tensor_tensor(out=BvT[:, ch, 0:N], in0=tht[th][:, 0:N], in1=ycb[:, c, 0:N], op=ALU.mult),
                        R=[Btht[th], Byc], W=[BBv[ch]])
                    yield "y"
            for hf in range(2):
                wt, bw = next_block("gb%d" % hf)
                for c in range(4):
                    o, Bo = fm_mm(wt, bw, c)
                    k.I("act", lambda e: e.activation(out=GBh[:, c, 0:N], in_=o[:, 0:N], func=AF.Tanh, scale=0.5), R=[Bo], W=[BGB])
                    yield "y"
                wt, bw = next_block("pb%d" % hf)
                for c in range(4):
                    ch = hf * 4 + c
                    o, Bo = tail_mm(wt, bw, c, BvT, BBv)
                    k.I("dve", lambda e: e.scalar_tensor_tensor(out=mT[:, ch, 0:N], in0=GBh[:, c, 0:N], scalar=1.0, in1=o[:, 0:N],
                                                                op0=ALU.add, op1=ALU.mult),
                        R=[BGB, Bo], W=[BmT[ch]])
                    yield "y"

        def gen_X(tiles, sx):
            scores, negm, bis, stp = scoresS[sx], negmS[sx], bisS[sx], stpS[sx]
            Bsc, Bnegm, Bbis = BscS[sx], BnegmS[sx], BbisS[sx]
            use_act = (sx == 1)
            for t in tiles:
                if seq < 2:
                    L = (p * 4 + t + 1) * 128
                    Lmin = L - 128
                    bisect = L > 256
                else:
                    L = past + 128
                    Lmin = past
                    bisect = True
                nkb = L // 128
                nb5 = (L + 511) // 512
                for h in range(8):
                    for b5 in range(nb5):
                        c0 = b5 * 512
                        n = min(512, L - c0)
                        i = rot("mm", 2)
                        pb_ = (h % 2) * 64
                        k.I("pe", lambda e: e.matmul(MM[i][:, 0:n], lhsT=qiT[pb_:pb_ + 64, h // 2, t * 128:(t + 1) * 128],
                                                     rhs=kiT[pb_:pb_ + 64, c0:c0 + n], start=True, stop=True),
                            R=[BqiT[t], BkiT], W=[BMM[i]])
                        r = rot("rl", 2)
                        k.I("act", lambda e: e.activation(out=rl[r][:, 0:n], in_=MM[i][:, 0:n], func=AF.Relu), R=[BMM[i]], W=[Brl[r]])
                        if h == 0:
                            k.I("dve", lambda e: e.tensor_scalar(scores[:, c0:c0 + n], rl[r][:, 0:n], wi_s[:, t, h:h + 1], None, op0=ALU.mult),
                                R=[Brl[r], Bwi], W=[Bsc[b5]])
                        else:
                            k.I("dve", lambda e: e.scalar_tensor_tensor(out=scores[:, c0:c0 + n], in0=rl[r][:, 0:n], scalar=wi_s[:, t, h:h + 1],
                                                                        in1=scores[:, c0:c0 + n], op0=ALU.mult, op1=ALU.add),
                                R=[Brl[r], Bwi, Bsc[b5]], W=[Bsc[b5]])
                        yield "idx"
                if seq < 2:
                    k.I("dve", lambda e: e.memset(scores[0:64, L - 64:L], NEGBIG), R=Bsc, W=Bsc)
                else:
                    k.I("dve", lambda e: e.memset(scores[:, past + DEC_SEQ:L], NEGBIG), R=Bsc, W=Bsc)
                if bisect:
                    k.I("dve", lambda e: e.tensor_reduce(out=bis[:, 0:1], in_=scores[:, 0:L], axis=AX.X, op=ALU.max), R=Bsc, W=[Bbis])
                    k.I("dve", lambda e: e.tensor_reduce(out=bis[:, 1:2], in_=scores[:, 0:Lmin], axis=AX.X, op=ALU.min), R=Bsc, W=[Bbis])
                    k.I("dve", lambda e: e.tensor_tensor(out=bis[:, 2:3], in0=bis[:, 0:1], in1=bis[:, 1:2], op=ALU.subtract), R=[Bbis], W=[Bbis])
                    k.I("dve", lambda e: e.tensor_scalar(stp[:], pf[:, O_POW2:O_POW2 + NIT + 2], bis[:, 2:3], None, op0=ALU.mult),
                        R=[Bbis, Bc], W=[Bbis])
                    k.I("dve", lambda e: e.tensor_tensor(out=bis[:, 3:4], in0=bis[:, 1:2], in1=stp[:, 2:3], op=ALU.add), R=[Bbis], W=[Bbis])
                    yield "bis"
                    if not use_act:
                        for it in range(1, NIT + 1):
                            k.I("dve", lambda e: e.tensor_scalar(junk[:, 0:L], scores[:, 0:L], bis[:, 3:4], 0.0, op0=ALU.is_ge, op1=ALU.add,
                                                                 accum_out=bis[:, 4:5]),
                                R=Bsc + [Bbis], W=[Bjunk, Bbis])
                            k.I("dve", lambda e: e.tensor_scalar(bis[:, 5:6], bis[:, 4:5], TOPK, 0.5, op0=ALU.is_ge, op1=ALU.subtract),
                                R=[Bbis], W=[Bbis])
                            if it < NIT:
                                k.I("dve", lambda e: e.scalar_tensor_tensor(out=bis[:, 3:4], in0=bis[:, 5:6], scalar=stp[:, it + 1:it + 2],
                                                                            in1=bis[:, 3:4], op0=ALU.mult, op1=ALU.add),
                                    R=[Bbis], W=[Bbis])
                            else:
                                k.I("dve", lambda e: e.tensor_scalar(bis[:, 5:6], bis[:, 5:6], 0.5, None, op0=ALU.subtract), R=[Bbis], W=[Bbis])
                                k.I("dve", lambda e: e.scalar_tensor_tensor(out=bis[:, 6:7], in0=bis[:, 5:6], scalar=stp[:, NIT + 1:NIT + 2],
                                                                            in1=bis[:, 3:4], op0=ALU.mult, op1=ALU.add),
                                    R=[Bbis], W=[Bbis])
                            yield "bis"
                    else:
                        k.I("dve", lambda e: e.tensor_scalar(bis[:, 7:8], bis[:, 3:4], -1.0, None, op0=ALU.mult), R=[Bbis], W=[Bbis])
                        for it in range(1, NIT + 1):
                            k.I("act", lambda e: e.activation(out=junk2[:, 0:L], in_=scores[:, 0:L], func=AF.Sign, bias=bis[:, 7:8], scale=1.0,
                                                              accum_out=bis[:, 4:5]),
                                R=Bsc + [Bbis], W=[Bjunk2, Bbis])
                            k.I("dve", lambda e: e.tensor_scalar(bis[:, 5:6], bis[:, 4:5], float(2 * TOPK - L), 0.5, op0=ALU.is_lt, op1=ALU.subtract),
                                R=[Bbis], W=[Bbis])
                            if it < NIT:
                                k.I("dve", lambda e: e.scalar_tensor_tensor(out=bis[:, 7:8], in0=bis[:, 5:6], scalar=stp[:, it + 1:it + 2],
                                                                            in1=bis[:, 7:8], op0=ALU.mult, op1=ALU.add),
                                    R=[Bbis], W=[Bbis])
                            else:
                                k.I("dve", lambda e: e.tensor_scalar(bis[:, 5:6], bis[:, 5:6], 0.5, None, op0=ALU.add), R=[Bbis], W=[Bbis])
                                k.I("dve", lambda e: e.scalar_tensor_tensor(out=bis[:, 6:7], in0=bis[:, 5:6], scalar=stp[:, NIT + 1:NIT + 2],
                                                                            in1=bis[:, 7:8], op0=ALU.mult, op1=ALU.add),
                                    R=[Bbis], W=[Bbis])
                                k.I("dve", lambda e: e.tensor_scalar(bis[:, 6:7], bis[:, 6:7], -1.0, None, op0=ALU.mult), R=[Bbis], W=[Bbis])
                            yield "bis"
                    k.I("dve", lambda e: e.tensor_scalar(negm[:, 0:L], scores[:, 0:L], bis[:, 6:7], NEG, op0=ALU.is_lt, op1=ALU.mult),
                        R=Bsc + [Bbis], W=[Bnegm])
                else:
                    k.I("dve", lambda e: e.tensor_scalar(negm[:, 0:L], scores[:, 0:L], -1.0e29, NEG, op0=ALU.is_lt, op1=ALU.mult),
                        R=Bsc, W=[Bnegm])
                yield "bis"
                yield "want_att"
                for g in range(2):
                    for kb in range(nkb):
                        si = rot("sp", 2)
                        k.I("pe", lambda e: e.matmul(SP_[si][:], lhsT=KT[:, g, kb * 128:(kb + 1) * 128],
                                                     rhs=QT[:, 4 * g:4 * g + 4, t * 128:(t + 1) * 128], start=True, stop=False),
                            R=[BKT, BQT[t]], W=[BSP[si]], sig=False)
                        k.I("pe", lambda e: e.matmul(SP_[si][:], lhsT=negm[:, kb * 128:(kb + 1) * 128],
                                                     rhs=I4.rearrange("p (h t) -> p h t", t=128), start=False, stop=True),
                            R=[Bnegm, Bc], W=[BSP[si]])
                        pi = rot("pt", 3)
                        k.I("act", lambda e: e.activation(out=PT[pi][:], in_=SP_[si][:], func=AF.Exp, scale=SCALE), R=[BSP[si]], W=[BPT[pi]])
                        k.I("pe", lambda e: e.matmul(OTp[:], lhsT=Vb[:, kb, g * 128:(g + 1) * 128], rhs=PT[pi][:],
                                                     start=(kb == 0), stop=(kb == nkb - 1)),
                            R=[BV, BPT[pi]], W=[BOT], sig=False)
                        k.I("pe", lambda e: e.matmul(SMp[:], lhsT=ones, rhs=PT[pi][:], start=(kb == 0), stop=(kb == nkb - 1)),
                            R=[Bc, BPT[pi]], W=[BSM], sig=True)
                        yield "att"
                    tf = rot("tf", 2)
                    t3 = tmpf[tf][:].rearrange("p (h t) -> p h t", t=128)
                    k.I("dve", lambda e: e.reciprocal(t3, SMp[:]), R=[BSM], W=[Btmpf[tf]])
                    k.I("dve", lambda e: e.tensor_tensor(out=t3, in0=OTp[:], in1=t3, op=ALU.mult), R=[BOT, Btmpf[tf]], W=[Btmpf[tf]])
                    k.I("dve", lambda e: e.scalar_tensor_tensor(out=zaT[:, 4 * g:4 * g + 4, t * 128:(t + 1) * 128], in0=t3, scalar=0.5,
                                                                in1=zaT[:, 4 * g:4 * g + 4, t * 128:(t + 1) * 128], op0=ALU.mult, op1=ALU.mult),
                        R=[Btmpf[tf]] + BzaT[4 * g:4 * g + 4], W=BzaT[4 * g:4 * g + 4])
                    yield "bis"
                yield "att_done"

        gz = gen_za()
        gy_main = gen_Y()

        def _chain():
            for x_ in gz:
                yield x_
            for x_ in gy_main:
                yield x_
        gy = _chain()
        ny_left = 64
        if seq < 2:
            nbt = sum(1 for t in range(NT) if (p * 4 + t + 1) * 128 > 256)
        else:
            nbt = 1
        nbis_left = nbt * (NIT + 2) + NT * 3
        st_y = {"done": False, "ny": ny_left, "nb": nbis_left}

        def y_steps():
            kq = 0 if st_y["done"] else max(1, -(-st_y["ny"] // max(1, st_y["nb"])))
            st_y["nb"] -= 1
            for _ in range(kq):
                try:
                    next(gy)
                    st_y["ny"] -= 1
                except StopIteration:
                    st_y["done"] = True
                    break

        streams = [gen_X([t for t in range(NT) if t % 2 == sx], sx) for sx in range(2)]
        want = [False, False]
        alive = [True, True]
        lock = None
        while any(alive):
            for sx in range(2):
                if not alive[sx]:
                    continue
                if want[sx]:
                    if lock is not None and lock != sx:
                        continue
                    lock = sx
                    want[sx] = False
                tag = next(streams[sx], None)
                if tag is None:
                    alive[sx] = False
                    if lock == sx:
                        lock = None
                elif tag == "want_att":
                    want[sx] = True
                    for _ in gz:
                        st_y["ny"] -= 1
                elif tag == "att_done":
                    if lock == sx:
                        lock = None
                elif tag == "bis":
                    y_steps()
        for _ in gy:
            pass

        if TRUNC == "xy":
            raise StopBuild()
        for hf in range(2):
            wt, bw = next_block("ga%d" % hf)
            for c in range(4):
                o, Bo = fm_mm(wt, bw, c)
                k.I("act", lambda e: e.activation(out=GAh[:, c, 0:N], in_=o[:, 0:N], func=AF.Tanh, scale=0.5), R=[Bo], W=[BGA])
            wt, bw = next_block("pa%d" % hf)
            for c in range(4):
                ch = hf * 4 + c
                o, Bo = tail_mm(wt, bw, c, zaT, BzaT)
                tf = rot("tf", 2)
                k.I("dve", lambda e: e.scalar_tensor_tensor(out=tmpf[tf][:, 0:N], in0=GAh[:, c, 0:N], scalar=1.0, in1=o[:, 0:N],
                                                            op0=ALU.add, op1=ALU.mult),
                    R=[BGA, Bo], W=[Btmpf[tf]])
                k.I("dve", lambda e: e.tensor_tensor(out=mT[:, ch, 0:N], in0=tmpf[tf][:, 0:N], in1=mT[:, ch, 0:N], op=ALU.add),
                    R=[Btmpf[tf], BmT[ch]], W=[BmT[ch]])
        for hf in range(2):
            wt, bw = next_block("out%d" % hf)
            for t in range(NT):
                xi = rot("xt", 2)
                k.dma("sp", xt[xi][0:nrow, 0:512], x_src(t)[:, hf * 512:(hf + 1) * 512], R=[Bin], W=[Bxt[xi]], sb=Bxt[xi])
                i = rot("mm", 2)
                for kc in range(8):
                    k.I("pe", lambda e, kc=kc: e.matmul(MM[i][:], lhsT=mT[:, kc, t * 128:(t + 1) * 128], rhs=wt[:, kc, :],
                                                        start=(kc == 0), stop=(kc == 7)),
                        R=[bw] + BmT, W=[BMM[i]], sig=(kc == 7))
                ys = rot("yst", 2)
                k.I("dve", lambda e: e.tensor_tensor(out=yst[ys][:], in0=MM[i][:], in1=gate_bc[:, hf * 512:(hf + 1) * 512], op=ALU.mult),
                    R=[BMM[i], Bgbc], W=[Byst[ys]])
                k.I("dve", lambda e: e.tensor_tensor(out=yst[ys][0:nrow, :], in0=yst[ys][0:nrow, :], in1=xt[xi][0:nrow, 0:512], op=ALU.add),
                    R=[Byst[ys], Bxt[xi]], W=[Byst[ys]])
                k.dma("act", outs["y"](t)[:, hf * 512:(hf + 1) * 512], yst[ys][0:nrow, :], R=[Byst[ys]], sb=Byst[ys])
        if last:
            k.dma("sp", outs["conv"], cvh[:], R=[Bcvh], sb=Bcvh)

    def seq_setup(seq):
        for hf in range(2):
            i = rot("mm", 2)
            k.I("pe", lambda e: e.matmul(MM[i][:], lhsT=sel[:, seq * 128:(seq + 1) * 128], rhs=gate_tm[:, hf * 512:(hf + 1) * 512],
                                         start=True, stop=True),
                R=[Bc, Bmod], W=[BMM[i]])
            k.I("act", lambda e: e.activation(out=gate_bc[:, hf * 512:(hf + 1) * 512], in_=MM[i][:], func=AF.Copy), R=[BMM[i]], W=[Bgbc])

    def _run_all():
        for seq in range(2):
            seq_setup(seq)
            k.I("dve", lambda e: e.memset(cvh[:], 0.0), W=[Bcvh])
            for p in range(4):
                outs = {
                    "kv": lambda t, seq=seq, p=p: (k_p[seq, (p * 4 + t) * 128:(p * 4 + t + 1) * 128, :],
                                                    v_p[seq, (p * 4 + t) * 128:(p * 4 + t + 1) * 128, :],
                                                    ki_p[seq, (p * 4 + t) * 128:(p * 4 + t + 1) * 128, :]),
                    "y": lambda t, seq=seq, p=p: y_p[seq, (p * 4 + t) * 128:(p * 4 + t + 1) * 128, :],
                    "conv": conv_p[seq],
                }
                seq_pass(seq, p, 4, lambda t, seq=seq, p=p: xp[seq, (p * 4 + t) * 128:(p * 4 + t + 1) * 128, :],
                         0, p * 4, outs, p == 3)

        seq_setup(2)
        k.dma("sp", cvh[:], sconv, R=[Bin], W=[Bcvh], sb=Bcvh)
        for half in range(2):
            k.dma("sp", cstg[:], ck.rearrange("(b p) c -> p b c", p=128), R=[Bin], W=Bcst, sb=Bcst[0]) if half == 0 else None
            if half == 0:
                for b in range(8):
                    qi_ = rot("qf", 4)
                    k.I("dve", lambda e, b=b: e.tensor_copy(qb[qi_][:, 0:256], cstg[:, b, :]), R=Bcst, W=[Bqb[qi_]])
                    ti = rot("tp", 2)
                    for h in range(2):
                        k.I("pe", lambda e, h=h: e.transpose(TP[ti][:, h * 128:(h + 1) * 128], qb[qi_][:, h * 128:(h + 1) * 128], ident),
                            R=[Bqb[qi_], Bc], W=[BTP[ti]], sig=(h == 1))
                    k.I("act", lambda e, b=b: e.activation(out=KT[:, 0:2, b * 128:(b + 1) * 128],
                                                           in_=TP[ti][:, 0:256].rearrange("p (h t) -> p h t", t=128), func=AF.Copy),
                        R=[BTP[ti]], W=[BKT])
            else:
                k.dma("sp", cstg[:], cvv.rearrange("(b p) c -> p b c", p=128), R=[Bin], W=Bcst, sb=Bcst[0])
                k.I("dve", lambda e: e.tensor_copy(Vb[:, 0:8, :], cstg[:]), R=Bcst, W=[BV])
        k.dma("sp", cstg[:, :, 0:64], cik.rearrange("(b p) c -> p b c", p=128), R=[Bin], W=Bcst, sb=Bcst[0])
        for b in range(8):
            qi_ = rot("qf", 4)
            k.I("dve", lambda e, b=b: e.tensor_copy(qb[qi_][:, 0:64], cstg[:, b, 0:64]), R=Bcst, W=[Bqb[qi_]])
            k.I("dve", lambda e, b=b: e.tensor_copy(qb[qi_][:, 64:128], cstg[:, b, 0:64]), R=Bcst, W=[Bqb[qi_]])
            ti = rot("tp", 2)
            k.I("pe", lambda e: e.transpose(TP[ti][:, 0:128], qb[qi_][:, 0:128], ident), R=[Bqb[qi_], Bc], W=[BTP[ti]])
            k.I("act", lambda e, b=b: e.activation(out=kiT[:, b * 128:(b + 1) * 128], in_=TP[ti][:, 0:128], func=AF.Copy),
                R=[BTP[ti]], W=[BkiT])
        outs = {
            "kv": lambda t: (k_s[:, :], v_s[:, :], ki_s[:, :]),
            "y": lambda t: y_s[:, :],
            "conv": conv_s,
        }
        seq_pass(2, 0, 1, lambda t: xs[:, :], PAST, 16, outs, True)


    try:
        _run_all()
    except StopBuild:
        pass
    k.E['pe'].pending = False
    k.finish()
    return nc


_PROG = {}


def _rope_tables():
    pos = np.zeros((NPT, 128), np.float32)
    for t in range(16):
        pos[t] = np.arange(t * 128, (t + 1) * 128, dtype=np.float32)
    pos[16] = np.arange(PAST, PAST + 128, dtype=np.float32)
    out = {}
    for rot_, name in ((32, "16"), (16, "8")):
        half = rot_ // 2
        inv = (np.float32(500000.0) ** (np.float32(-2.0) * np.arange(half, dtype=np.float32) / np.float32(rot_))).astype(np.float32)
        ang = (pos[:, :, None] * inv[None, None, :]).astype(np.float32)
        out["cos" + name] = np.cos(ang).astype(np.float32).transpose(1, 0, 2).reshape(128, NPT * half)
        out["sin" + name] = np.sin(ang).astype(np.float32).transpose(1, 0, 2).reshape(128, NPT * half)
    return out


def kernel(x_prompt, x_sample, cache_k, cache_v, cache_idx_k, state_conv, c_prompt, c_sample,
           w_ada, b_ada, norm_g, w_in, q_norm_g, k_norm_g, conv_w, w_pa, w_pb, w_out):
    f = lambda a: np.ascontiguousarray(np.asarray(a, dtype=np.float32))
    x_prompt, x_sample, cache_k, cache_v, cache_idx_k, state_conv = map(f, (x_prompt, x_sample, cache_k, cache_v, cache_idx_k, state_conv))
    c_prompt, c_sample, w_ada, b_ada, norm_g, w_in = map(f, (c_prompt, c_sample, w_ada, b_ada, norm_g, w_in))
    q_norm_g, k_norm_g, conv_w, w_pa, w_pb, w_out = map(f, (q_norm_g, k_norm_g, conv_w, w_pa, w_pb, w_out))

    if "nc" not in _PROG:
        _PROG["nc"] = build_program()
    nc = _PROG["nc"]

    rt_ = _rope_tables()
    pf = np.zeros((128, NPF), np.float32)
    pf[:, O_COS16:O_COS16 + NPT * 16] = rt_["cos16"]
    pf[:, O_SIN16:O_SIN16 + NPT * 16] = rt_["sin16"]
    pf[:, O_COS8:O_COS8 + NPT * 8] = rt_["cos8"]
    pf[:, O_SIN8:O_SIN8 + NPT * 8] = rt_["sin8"]
    pf[:, O_POW2:O_POW2 + NIT + 2] = (2.0 ** (1.0 - np.arange(NIT + 2, dtype=np.float64))).astype(np.float32)[None, :]
    pf[:, O_GQ:O_GQ + 128] = q_norm_g[None, :]
    pf[:, O_GK:O_GK + 128] = k_norm_g[None, :]
    pf[:, O_BADA:O_BADA + 24] = b_ada.reshape(24, 128).T
    pf[:, O_NG:O_NG + 8] = norm_g.reshape(8, 128).T
    pf[:, O_CW:O_CW + 24] = conv_w.reshape(3, 8, 128).transpose(2, 0, 1).reshape(128, 24)
    pf[:, O_NH:O_NH + 16] = -0.5
    pb = np.zeros((128, 768), np.float32)
    eye = np.eye(128, dtype=np.float32)
    pb[:, 0:128] = eye
    pb[:, 128:640] = np.tile(eye, (1, 4))
    pb[:, 640:768] = 1.0
    pb = pb.astype(ml_dtypes.bfloat16)
    sel = np.zeros((3, 3, 128), np.float32)
    for s in range(3):
        sel[s, s, :] = 1.0
    sel = np.ascontiguousarray(sel.transpose(1, 0, 2).reshape(3, 384))
    bgate = np.ascontiguousarray(np.broadcast_to(b_ada[2048:3072][None, :], (3, D)))

    in_maps = []
    for c in range(8):
        cc = np.stack([c_prompt[2 * c], c_prompt[2 * c + 1], c_sample[c]], axis=0)
        cTm = np.ascontiguousarray(cc.reshape(3, 8, 128).transpose(2, 1, 0))
        sc = np.ascontiguousarray(state_conv[c].reshape(2, 8, 128).transpose(2, 1, 0))
        in_maps.append({
            "xp": np.ascontiguousarray(x_prompt[2 * c:2 * c + 2]),
            "xs": np.ascontiguousarray(x_sample[c]),
            "ck": np.ascontiguousarray(cache_k[c].reshape(PAST, 256)),
            "cvv": np.ascontiguousarray(cache_v[c].reshape(PAST, 256)),
            "cik": np.ascontiguousarray(cache_idx_k[c]),
            "sconv": sc, "cT": cTm,
            "w_ada": w_ada, "w_in": w_in, "w_pa": w_pa, "w_pb": w_pb, "w_out": w_out,
            "pf32": pf, "pbf": pb, "sel": sel, "bgate": bgate,
        })
    res = run_bass_kernel_spmd(nc, in_maps, core_ids=list(range(8)))
    R = res.results
    y_p = np.concatenate([R[c]["y_p"] for c in range(8)], axis=0)
    y_s = np.stack([R[c]["y_s"] for c in range(8)], axis=0)
    k_p = np.concatenate([R[c]["k_p"] for c in range(8)], axis=0).reshape(16, SEQ, 2, 128)
    v_p = np.concatenate([R[c]["v_p"] for c in range(8)], axis=0).reshape(16, SEQ, 2, 128)
    ki_p = np.concatenate([R[c]["ki_p"] for c in range(8)], axis=0)
    conv_p = np.concatenate([R[c]["conv_p"] for c in range(8)], axis=0)
    conv_p = np.ascontiguousarray(conv_p.transpose(0, 3, 2, 1).reshape(16, 2, D))
    k_s = np.stack([R[c]["k_s"] for c in range(8)], axis=0).reshape(8, DEC_SEQ, 2, 128)
    v_s = np.stack([R[c]["v_s"] for c in range(8)], axis=0).reshape(8, DEC_SEQ, 2, 128)
    ki_s = np.stack([R[c]["ki_s"] for c in range(8)], axis=0)
    conv_s = np.stack([R[c]["conv_s"] for c in range(8)], axis=0)
    conv_s = np.ascontiguousarray(conv_s.transpose(0, 3, 2, 1).reshape(8, 2, D))
    f32 = lambda a: np.ascontiguousarray(a, dtype=np.float32)
    return tuple(map(f32, (y_p, y_s, k_p, v_p, ki_p, conv_p, k_s, v_s, ki_s, conv_s)))
```
